# Optimizing a Trainium2 kernel written in Bass

```python
import math
import jax, jax.numpy as jnp
from jax import lax
import numpy as np

D_MODEL = 1024
BATCH = 8
SEQ = 4096
DEPTH = 4

N_MEM = 256
D_MIX = D_MODEL
RET_WIDTH = D_MIX // 2
RET_HEADS = 4
RET_HEAD_DIM = RET_WIDTH // RET_HEADS
POOL_WIDTH = D_MIX - RET_WIDTH
POOL_WINDOWS = (2, 4, 8, 16)
POOL_GROUP = POOL_WIDTH // len(POOL_WINDOWS)
IN_COLS = 4 * RET_WIDTH + POOL_WIDTH
CHUNK = 128
ROPE_BASE = 10000.0
XA_HEADS = 4
XA_HEAD_DIM = D_MODEL // XA_HEADS
D_FF = ((8 * D_MODEL // 3 + 255) // 256) * 256
CONV_WIDTH = 3
EPS = 1e-6

kernel_name = "hybrid_retention_pool_trunk"


def rmsnorm(x, g):
    x32 = x.astype(jnp.float32)
    y = x32 * lax.rsqrt(jnp.mean(x32 * x32, axis=-1, keepdims=True) + EPS)
    return (y * g.astype(jnp.float32)).astype(x.dtype)


def rope(t, positions):
    dh = t.shape[-1]
    inv_freq = jnp.exp(-math.log(ROPE_BASE) * jnp.arange(0, dh, 2, dtype=jnp.float32) / dh)
    ang = positions.astype(jnp.float32)[..., None] * inv_freq
    cos = jnp.cos(ang)[:, :, None, :].astype(t.dtype)
    sin = jnp.sin(ang)[:, :, None, :].astype(t.dtype)
    t1, t2 = jnp.split(t, 2, axis=-1)
    return jnp.concatenate([t1 * cos - t2 * sin, t2 * cos + t1 * sin], axis=-1)


def retention(q, k, v, positions):
    B, S, _ = q.shape
    H, Dh = RET_HEADS, RET_HEAD_DIM
    dt = q.dtype
    N = S // CHUNK
    q = rope(q.reshape(B, S, H, Dh), positions)
    k = rope(k.reshape(B, S, H, Dh), positions) * (Dh ** -0.5)
    v = v.reshape(B, S, H, Dh)

    def to_chunks(t):
        return t.reshape(B, N, CHUNK, H, Dh).transpose(0, 3, 1, 2, 4)

    qc, kc, vc = to_chunks(q), to_chunks(k), to_chunks(v)
    lg = jnp.log1p(-jnp.exp2(-5.0 - jnp.arange(H, dtype=jnp.float32)))
    idx = jnp.arange(CHUNK, dtype=jnp.float32)
    diff = idx[:, None] - idx[None, :]
    dmat = jnp.where(diff >= 0, jnp.exp(lg[:, None, None] * jnp.maximum(diff, 0.0)), 0.0).astype(dt)
    zeta = jnp.exp(lg[:, None] * (CHUNK - 1 - idx)).astype(dt)
    xi = jnp.exp(lg[:, None] * (idx + 1.0)).astype(dt)
    chunk_decay = jnp.exp(lg * CHUNK).astype(dt)

    scores = jnp.einsum('bhnid,bhnjd->bhnij', qc, kc) * dmat[:, None]
    intra = jnp.einsum('bhnij,bhnjv->bhniv', scores, vc)
    kv = jnp.einsum('bhnjd,bhnjv->nbhdv', kc * zeta[:, None, :, None], vc)

    def step(state, kv_n):
        return state * chunk_decay[None, :, None, None] + kv_n, state

    _, states_before = lax.scan(step, jnp.zeros((B, H, Dh, Dh), dt), kv)
    cross = jnp.einsum('bhnid,nbhdv->bhniv', qc, states_before) * xi[:, None, :, None]
    return (intra + cross).transpose(0, 2, 3, 1, 4).reshape(B, S, H, Dh)


def head_rmsnorm(y, g):
    B, S, H, Dh = y.shape
    y32 = y.astype(jnp.float32)
    y32 = y32 * lax.rsqrt(jnp.mean(y32 * y32, axis=-1, keepdims=True) + EPS)
    return (y32.reshape(B, S, H * Dh) * g.astype(jnp.float32)).astype(y.dtype)


def pool_mixer(u, w, scale):
    B, S, _ = u.shape
    u32 = u.astype(jnp.float32)
    cs = jnp.cumsum(u32, axis=1)
    t1 = jnp.arange(1, S + 1, dtype=jnp.int32)
    groups = []
    for gi, wlen in enumerate(POOL_WINDOWS):
        c = cs[:, :, gi * POOL_GROUP:(gi + 1) * POOL_GROUP]
        lower = jnp.concatenate([jnp.zeros((B, wlen, POOL_GROUP), jnp.float32), c[:, :S - wlen]], axis=1)
        count = jnp.minimum(t1, wlen).astype(jnp.float32)[None, :, None]
        mean = (c - lower) / count
        groups.append(mean - u32[:, :, gi * POOL_GROUP:(gi + 1) * POOL_GROUP])
    pooled = jnp.stack(groups, axis=2).astype(u.dtype)
    out = jnp.einsum('bsgc,gcd->bsgd', pooled, w).reshape(B, S, POOL_WIDTH)
    return out * scale


def cross_attn(h, mem_n, wq, wkv, wo):
    B, S, _ = h.shape
    q = (h @ wq).reshape(B, S, XA_HEADS, XA_HEAD_DIM)
    k, v = jnp.split(mem_n @ wkv, 2, axis=-1)
    k = k.reshape(B, N_MEM, XA_HEADS, XA_HEAD_DIM)
    v = v.reshape(B, N_MEM, XA_HEADS, XA_HEAD_DIM)
    s = jnp.einsum('bshd,bmhd->bhsm', q, k).astype(jnp.float32) * (XA_HEAD_DIM ** -0.5)
    p = jax.nn.softmax(s, axis=-1).astype(v.dtype)
    o = jnp.einsum('bhsm,bmhd->bshd', p, v).reshape(B, S, D_MODEL)
    return o @ wo


def conv_glu_ffn(h, w_up, conv_w, conv_b, w_down):
    a = h @ w_up
    S = a.shape[1]
    a1 = jnp.pad(a, ((0, 0), (1, 0), (0, 0)))[:, :S]
    a2 = jnp.pad(a, ((0, 0), (2, 0), (0, 0)))[:, :S]
    c = conv_w[0] * a2 + conv_w[1] * a1 + conv_w[2] * a + conv_b
    gate, val = jnp.split(c, 2, axis=-1)
    return (jax.nn.silu(gate) * val) @ w_down


def setup_inputs(seed: int = 0) -> dict:
    key = jax.random.key(seed)
    ks = jax.random.split(key, 24)

    def nrm(k, shape, scale):
        return jax.random.normal(k, shape, jnp.float32) * scale

    L = DEPTH
    x = nrm(ks[0], (BATCH, SEQ, D_MODEL), 1.0)
    mem = nrm(ks[1], (BATCH, N_MEM, D_MODEL), 1.0)
    offs = jax.random.randint(ks[2], (BATCH, 1), 0, 1024, dtype=jnp.int32)
    positions = (offs + jnp.arange(SEQ, dtype=jnp.int32)[None, :]).astype(jnp.int32)
    return {
        "x": x,
        "mem": mem,
        "positions": positions,
        "mix_norm_g": 1.0 + nrm(ks[3], (L, D_MODEL), 0.02),
        "w_in": nrm(ks[4], (L, D_MODEL, IN_COLS), D_MODEL ** -0.5),
        "ret_gn_g": 1.0 + nrm(ks[5], (L, RET_WIDTH), 0.02),
        "pool_w": nrm(ks[6], (L, len(POOL_WINDOWS), POOL_GROUP, POOL_GROUP), POOL_GROUP ** -0.5),
        "pool_scale": 0.5 + nrm(ks[7], (L, POOL_WIDTH), 0.1),
        "w_out": nrm(ks[8], (L, D_MIX, D_MODEL), (2.0 * D_MIX) ** -0.5),
        "xa_norm_g": 1.0 + nrm(ks[9], (L, D_MODEL), 0.02),
        "mem_norm_g": 1.0 + nrm(ks[10], (D_MODEL,), 0.02),
        "xa_wq": nrm(ks[11], (L, D_MODEL, D_MODEL), D_MODEL ** -0.5),
        "xa_wkv": nrm(ks[12], (L, D_MODEL, 2 * D_MODEL), D_MODEL ** -0.5),
        "xa_wo": nrm(ks[13], (L, D_MODEL, D_MODEL), (2.0 * D_MODEL) ** -0.5),
        "ffn_norm_g": 1.0 + nrm(ks[14], (L, D_MODEL), 0.02),
        "ffn_w_up": nrm(ks[15], (L, D_MODEL, 2 * D_FF), D_MODEL ** -0.5),
        "ffn_conv_w": nrm(ks[16], (L, CONV_WIDTH, 2 * D_FF), CONV_WIDTH ** -0.5),
        "ffn_conv_b": nrm(ks[17], (L, 2 * D_FF), 0.01),
        "ffn_w_down": nrm(ks[18], (L, D_FF, D_MODEL), (2.0 * D_FF) ** -0.5),
        "final_norm_g": 1.0 + nrm(ks[19], (D_MODEL,), 0.02),
    }


def reference(x, mem, positions, mix_norm_g, w_in, ret_gn_g, pool_w, pool_scale, w_out,
              xa_norm_g, mem_norm_g, xa_wq, xa_wkv, xa_wo, ffn_norm_g, ffn_w_up,
              ffn_conv_w, ffn_conv_b, ffn_w_down, final_norm_g):
    mem_n = rmsnorm(mem, mem_norm_g)
    R = RET_WIDTH
    for l in range(DEPTH):
        h = rmsnorm(x, mix_norm_g[l])
        proj = h @ w_in[l]
        q, k, v, g, u = jnp.split(proj, [R, 2 * R, 3 * R, 4 * R], axis=-1)
        y_ret = jax.nn.silu(g) * head_rmsnorm(retention(q, k, v, positions), ret_gn_g[l])
        y_pool = pool_mixer(u, pool_w[l], pool_scale[l])
        x = x + jnp.concatenate([y_ret, y_pool], axis=-1) @ w_out[l]
        x = x + cross_attn(rmsnorm(x, xa_norm_g[l]), mem_n, xa_wq[l], xa_wkv[l], xa_wo[l])
        x = x + conv_glu_ffn(rmsnorm(x, ffn_norm_g[l]), ffn_w_up[l], ffn_conv_w[l],
                             ffn_conv_b[l], ffn_w_down[l])
    return rmsnorm(x, final_norm_g)
```

```python
import math
from contextlib import ExitStack

import numpy as np
import concourse.bass as bass
import concourse.mybir as mybir
from concourse.bass_utils import run_bass_kernel_spmd

F32 = mybir.dt.float32
BF16 = mybir.dt.bfloat16
I32 = mybir.dt.int32
ALU = mybir.AluOpType
AF = mybir.ActivationFunctionType

D = 1024
NMEM = 256
DFF = 2816
NFC = 22
EPS = 1e-6
TT = 512
FUSED = True
LABELS = False
NSLOT = 5
SLOTW = 4096
WINDOWS = (2, 4, 8, 16)
GAM = [1.0 - 2.0 ** (-5 - h) for h in range(4)]
GAMC = [float(np.float32(np.exp(np.float32(np.log1p(-np.float32(2.0 ** (-5 - h)))) * 128))) for h in range(4)]


class Res:
    __slots__ = ("name", "w", "rd")

    def __init__(self, name):
        self.name = name
        self.w = None
        self.rd = {}


class _Eng:
    def __init__(self, name, handle, sem, step):
        self.name = name
        self.handle = handle
        self.sem = sem
        self.step = step
        self.cnt = 0
        self.seen = {}
        self.hist = {}


class Sync:
    def __init__(self, nc, stack):
        self.nc = nc
        self.stack = stack
        self.engs = {}
        self.nwaits = 0
        self.ninstr = 0
        self.label = ""
        self.labels = {}
        for name, h in (("pe", nc.tensor), ("act", nc.scalar), ("dve", nc.vector),
                        ("pool", nc.gpsimd), ("sp", nc.sync)):
            sem = stack.enter_context(nc.semaphore("s_" + name))
            self.engs[name] = _Eng(name, h, sem, 1)

    def dma_stream(self, name):
        sem = self.stack.enter_context(self.nc.semaphore("d_" + name))
        e = _Eng("dma:" + name, None, sem, 16)
        self.engs[e.name] = e
        return e.name

    def _deps(self, eng, reads, writes):
        deps = {}

        def add(p, v, raw):
            if p == eng and (eng == "pe" or not raw):
                return
            if deps.get(p, 0) < v:
                deps[p] = v

        for r in reads:
            if r.w is not None:
                add(r.w[0], r.w[1], True)
        for r in writes:
            if r.w is not None:
                add(r.w[0], r.w[1], False)
            for p, v in r.rd.items():
                add(p, v, False)
        return deps

    def _wait(self, e, deps):
        for p, v in deps.items():
            if e.seen.get(p, 0) >= v:
                continue
            pe = self.engs[p]
            e.handle.wait_ge(pe.sem, v)
            self.nwaits += 1
            snap = pe.hist.get(v)
            if snap is None:
                snap = pe.seen
            for q, u in snap.items():
                if e.seen.get(q, 0) < u:
                    e.seen[q] = u
            e.seen[p] = v

    def op(self, eng, emit, reads=(), writes=(), inc=True):
        e = self.engs[eng]
        self._wait(e, self._deps(eng, reads, writes))
        ins = emit(e.handle)
        self.ninstr += 1
        if LABELS:
            self.labels[ins.ins.name] = self.label
        val = (e.cnt + 1) * e.step
        if inc:
            ins.then_inc(e.sem, e.step)
            e.cnt += 1
            e.hist[val] = dict(e.seen)
        for r in reads:
            if r.rd.get(eng, 0) < val:
                r.rd[eng] = val
        for r in writes:
            r.w = (eng, val)
            r.rd = {}
        return ins

    def dma(self, queue, stream, out, in_, reads=(), writes=()):
        q = self.engs[queue]
        d = self.engs[stream]
        self._wait(q, self._deps(stream, reads, writes))
        ins = q.handle.dma_start(out=out, in_=in_)
        ins.then_inc(d.sem, 16)
        self.ninstr += 1
        d.cnt += 1
        val = d.cnt * 16
        d.hist[val] = dict(q.seen)
        d.seen = dict(q.seen)
        for r in reads:
            if r.rd.get(stream, 0) < val:
                r.rd[stream] = val
        for r in writes:
            r.w = (stream, val)
            r.rd = {}
        return ins


def _consts():
    cols = {}
    parts = []
    off = 0

    def add(name, arr):
        nonlocal off
        arr = np.asarray(arr, np.float32)
        assert arr.shape[0] == 128
        cols[name] = (off, arr.shape[1])
        parts.append(arr)
        off += arr.shape[1]

    idx = np.arange(128)
    add("ident", np.eye(128))
    add("ones", np.ones((128, 128)))
    m = (idx[:, None] <= idx[None, :]).astype(np.float32)
    add("mask", m)
    band, bandp, band0 = [], [], []
    for w in WINDOWS:
        t = idx[:, None]
        j = idx[None, :]
        B = ((j >= t - w + 1) & (j <= t)).astype(np.float64) / w - (j == t)
        Bp = ((j - 128 >= t - w + 1) & (j - 128 <= -1)).astype(np.float64) / w
        cnt = np.minimum(t + 1, w)
        B0 = ((j >= np.maximum(t - w + 1, 0)) & (j <= t)).astype(np.float64) / cnt - (j == t)
        band.append(B.T)
        bandp.append(Bp.T)
        band0.append(B0.T)
    add("band", np.concatenate(band, 1))
    add("bandp", np.concatenate(bandp, 1))
    add("band0", np.concatenate(band0, 1))
    lg = np.log1p(-np.exp2(-5.0 - np.arange(4, dtype=np.float64)))
    i1 = (idx + 1.0)[:, None]
    add("dq", np.exp(lg[None, :] * i1))
    add("dk", np.exp(-lg[None, :] * i1) * (128.0 ** -0.5))
    invf = np.exp(-math.log(10000.0) * np.arange(0, 128, 2, dtype=np.float32) / 128).astype(np.float32)
    add("invf", np.tile(invf[None, :], (128, 1)))
    return np.concatenate(parts, 1).astype(np.float32), cols


def build(S, NL, final_norm=True):
    assert S % TT == 0
    NT = S // TT
    NCH = S // 128
    cblob, ccols = _consts()
    NC_ = cblob.shape[1]

    nc = bass.Bass("TRN2", target_bir_lowering=False)
    dr = lambda name, shape, dt=F32, kind="ExternalInput": nc.dram_tensor(name, shape, dt, kind=kind).ap()
    xT_d = dr("xT", [D, S])
    memT_d = dr("memT", [D, NMEM])
    pos_d = dr("pos", [128, NCH], I32)
    const_d = dr("consts", [128, NC_])
    gvec_d = dr("gvec", [128, (3 * NL + 2) * 8])
    gn_d = dr("gn", [NL, 128, 512])
    psc_d = dr("psc", [128, NL * 4])
    cw_d = dr("cw", [128, NL * 4 * 44])
    w_in_d = dr("w_in", [NL, 5, 128, SLOTW])
    w_out_d = dr("w_out", [NL, 2, 128, SLOTW])
    wq_d = dr("wq", [NL, 2, 128, SLOTW])
    wo_d = dr("wo", [NL, 2, 128, SLOTW])
    wkv_d = dr("wkv", [NL, 4, 128, SLOTW])
    w_up_d = dr("w_up", [NL, 11, 128, SLOTW])
    w_dn_d = dr("w_dn", [NL, 8, 128, 3072])
    pw_d = dr("pw", [NL, 128, 512])
    out_d = dr("out", [D, S], kind="ExternalOutput")

    with ExitStack() as st:
        S_ = Sync(nc, st)
        sbuf = lambda name, shape, dt: st.enter_context(nc.sbuf_tensor(name, shape, dt))
        psum = lambda name, shape, dt: st.enter_context(nc.psum_tensor(name, shape, dt))

        xT = sbuf("xT_sb", [128, 8, TT], F32)
        xR = [Res(f"xT{k}") for k in range(8)]
        hT = sbuf("hT", [128, 8, TT], BF16)
        hR = [Res(f"hT{k}") for k in range(8)]
        big = sbuf("big", [128, NFC, TT], BF16)
        bR = [Res(f"big{k}") for k in range(NFC)]
        bR2 = [Res(f"bigh{k}") for k in range(8)]

        def BR(k0, k1=None):
            ks = range(k0, k1) if k1 is not None else [k0]
            out = []
            for k in ks:
                out.append(bR[k])
                if k < 8:
                    out.append(bR2[k])
            return out

        def BRh(k0, k1, half):
            return [(bR if half == 0 else bR2)[k] for k in range(k0, k1)]
        KT = sbuf("KT", [128, NL * 8, NMEM], BF16)
        Vm = sbuf("Vm", [128, NL * 2, D], BF16)
        kvR = [Res(f"kv{l}") for l in range(NL)]
        Tst = sbuf("Tst", [128, NL, 512], F32)
        SbfL = sbuf("SbfL", [128, NL, 512], BF16)
        uprev = sbuf("uprev", [128, NL, 512], BF16)
        atail = sbuf("atail", [128, NL * 44, 2], F32)
        stT = [Res(f"stT{l}") for l in range(NL)]
        stS = [Res(f"stS{l}") for l in range(NL)]
        stU = [Res(f"stU{l}") for l in range(NL)]
        tailR = [[Res(f"tail{l}_{c}") for c in range(44)] for l in range(NL)]
        NCF = 128 + 4 + 4 + 64
        cst = sbuf("cst", [128, NCF], F32)
        cR = Res("cst")
        cbf = sbuf("cbf", [128, 256 + 1536], BF16)
        gvec = sbuf("gvec_sb", [128, (3 * NL + 2) * 8], F32)
        psc = sbuf("psc_sb", [128, NL * 4], F32)
        cw = sbuf("cw_sb", [128, NL * 4 * 44], F32)
        gn = sbuf("gn_sb", [128, 512], F32)
        gnR = Res("gn")
        epsT = sbuf("epsT", [128, 1], F32)
        posi = sbuf("posi", [128, NCH], I32)
        posf = sbuf("posf", [128, NCH], F32)
        cosT = sbuf("cosT", [128, 4, 64], F32)
        sinT = sbuf("sinT", [128, 4, 64], F32)
        ropeR = Res("rope")
        slots = [sbuf(f"slot{i}", [128, SLOTW], BF16) for i in range(NSLOT)]
        slotR = [Res(f"slot{i}") for i in range(NSLOT)]
        pwb = [sbuf(f"pwb{i}", [128, 512], BF16) for i in range(2)]
        pwR = [Res(f"pwb{i}") for i in range(2)]

        ident = cbf[:, 0:128]
        ones = cbf[:, 128:256]
        bandb = cbf[:, 256:768]
        bandpb = cbf[:, 768:1280]
        band0b = cbf[:, 1280:1792]
        maskf = cst[:, 0:128]
        dq = cst[:, 128:132]
        dk = cst[:, 132:136]
        invf = cst[:, 136:200]

        rt = sbuf("rt", [128, TT], F32)
        rstd = sbuf("rstd", [128, TT], F32)
        rtR, rstdR = Res("rt"), Res("rstd")
        ARENA_KB = 52
        arena = sbuf("arena", [128, ARENA_KB * 512], BF16)
        aR = [Res(f"ar{i}") for i in range(ARENA_KB)]

        class Buf:
            def __init__(self, ap, res):
                self.ap = ap
                self.res = res

        class Arena:
            def __init__(self):
                self.off = 0

            def reset(self):
                self.off = 0

            def alloc(self, ncols, dt):
                nb = ncols * (4 if dt == F32 else 2)
                nb = (nb + 1023) // 1024 * 1024
                o = self.off
                self.off += nb
                assert self.off <= ARENA_KB * 1024, "arena overflow"
                ap = arena[:, o // 2:(o + nb) // 2]
                if dt == F32:
                    ap = ap.bitcast(F32)
                ap = ap[:, 0:ncols]
                res = aR[o // 1024:(o + nb - 1) // 1024 + 1]
                return Buf(ap, list(res))

        AR = Arena()
        sq = arena[:, (ARENA_KB - 8) * 512:ARENA_KB * 512].rearrange("p (k t) -> p k t", k=8)
        sqRl = aR[ARENA_KB - 8:ARENA_KB]

        banks = [psum(f"bank{i}", [128, 512], F32) for i in range(8)]
        bankR = [Res(f"bank{i}") for i in range(8)]
        bank_i = [0]

        def nbank():
            i = bank_i[0] % 8
            bank_i[0] += 1
            return banks[i], bankR[i]

        d_in = S_.dma_stream("in")
        d_mem = S_.dma_stream("mem")
        d_cb = S_.dma_stream("cb")
        d_x = S_.dma_stream("x")
        d_out = S_.dma_stream("out")
        d_slot = [S_.dma_stream(f"sl{i}") for i in range(NSLOT)]
        d_pw = [S_.dma_stream(f"pw{i}") for i in range(2)]
        d_gn = S_.dma_stream("gn")

        OP = S_.op

        wq_list = []
        w_next = [0]
        w_use = [0]

        def w_issue():
            i = w_next[0]
            if i >= len(wq_list):
                return False
            src, ncols = wq_list[i]
            s = i % NSLOT
            S_.dma("pool", d_slot[s], slots[s][:, 0:ncols], src, writes=[slotR[s]])
            w_next[0] += 1
            return True

        def w_get():
            i = w_use[0]
            while w_next[0] <= i:
                assert w_issue()
            w_use[0] += 1
            s = i % NSLOT
            return slots[s], slotR[s]

        def w_prefetch():
            while w_next[0] < len(wq_list) and w_next[0] < w_use[0] + NSLOT:
                w_issue()

        o0 = ccols["mask"][0]
        S_.dma("sp", d_in, cst[:, 0:128], const_d[:, o0:o0 + 128], writes=[cR])
        o0 = ccols["dq"][0]
        S_.dma("sp", d_in, cst[:, 128:200], const_d[:, o0:o0 + 72], writes=[cR])
        S_.dma("sp", d_in, gvec[:], gvec_d, writes=[cR])
        S_.dma("sp", d_in, psc[:], psc_d, writes=[cR])
        S_.dma("sp", d_in, cw[:], cw_d, writes=[cR])
        S_.dma("sp", d_in, posi[:], pos_d, writes=[cR])
        cbR = Res("cbf")
        OP("dve", lambda e: e.memset(epsT[:], EPS), writes=[cR])
        o0 = ccols["ident"][0]
        S_.dma("pool", d_cb, cbf[:, 0:256], const_d[:, o0:o0 + 256], writes=[cbR])
        o0 = ccols["band"][0]
        S_.dma("pool", d_cb, cbf[:, 256:1792], const_d[:, o0:o0 + 1536], writes=[cbR])
        OP("dve", lambda e: e.tensor_copy(out=posf[:], in_=posi[:]), reads=[cR], writes=[cR])
        for l in range(NL):
            OP("dve", lambda e, l=l: e.memset(Tst[:, l, :], 0.0), writes=[stT[l]])
            OP("pool", lambda e, l=l: e.memset(SbfL[:, l, :], 0.0), writes=[stS[l]])
            OP("pool", lambda e, l=l: e.memset(uprev[:, l, :], 0.0), writes=[stU[l]])
        OP("dve", lambda e: e.memset(atail[:], 0.0), writes=[r for l in range(NL) for r in tailR[l]])

        for l in range(NL):
            for p in range(4):
                wq_list.append((wkv_d[l, p], SLOTW))
        for t in range(NT):
            for l in range(NL):
                for p in range(5):
                    wq_list.append((w_in_d[l, p], SLOTW))
                for p in range(2):
                    wq_list.append((w_out_d[l, p], SLOTW))
                for p in range(2):
                    wq_list.append((wq_d[l, p], SLOTW))
                for p in range(2):
                    wq_list.append((wo_d[l, p], SLOTW))
                for p in range(11):
                    wq_list.append((w_up_d[l, p], SLOTW))
                for p in range(4):
                    wq_list.append((w_dn_d[l, p], 3072))
                for p in range(4):
                    wq_list.append((w_dn_d[l, 4 + p, :, 0:2560], 2560))
        w_prefetch()

        def gcol(idx, kc):
            return gvec[:, idx * 8 + kc: idx * 8 + kc + 1]

        def mm_group(out_ap, bres, pairs, reads):
            n = len(pairs)
            for i, (l_, r_) in enumerate(pairs):
                OP("pe", lambda e, l_=l_, r_=r_, i=i: e.matmul(out_ap, lhsT=l_, rhs=r_, start=(i == 0), stop=(i == n - 1)),
                   reads=reads, writes=[bres], inc=(i == n - 1))

        def norm(src, srcR, gidx, dst, dstR, N, in_place_f32=False):
            for kc in range(8):
                OP("act", lambda e, kc=kc: e.activation(out=sq[:, kc, 0:N], in_=src[:, kc, 0:N], func=AF.Square),
                   reads=[srcR[kc]], writes=[sqRl[kc]])
            bk, br = nbank()
            mm_group(bk[:, 0:N], br, [(ones, sq[:, kc, 0:N]) for kc in range(8)], sqRl + [cbR])
            OP("act", lambda e: e.activation(out=rt[:, 0:N], in_=bk[:, 0:N], func=AF.Ln, bias=epsT[:, 0:1], scale=1.0 / D),
               reads=[br, cR], writes=[rtR])
            OP("act", lambda e: e.activation(out=rstd[:, 0:N], in_=rt[:, 0:N], func=AF.Exp, scale=-0.5), reads=[rtR], writes=[rstdR])
            for kc in range(8):
                eng = "dve"
                OP(eng, lambda e, kc=kc: e.scalar_tensor_tensor(out=dst[:, kc, 0:N], in0=src[:, kc, 0:N], scalar=gcol(gidx, kc),
                                                                in1=rstd[:, 0:N], op0=ALU.mult, op1=ALU.mult),
                   reads=[srcR[kc], rstdR, cR], writes=[dstR[kc]])

        rtok = sbuf("rtok", [128, 40], F32)
        rtokR = Res("rtok")
        rtokQ = [Res(f"rtokq{c}") for c in range(4)]
        rtokK = [Res(f"rtokk{c}") for c in range(4)]

        def norm_h(gidx):
            for kc in range(8):
                if kc % 2 == 0:
                    OP("dve", lambda e, kc=kc: e.tensor_scalar(out=hT[:, kc, :], in0=xT[:, kc, :], scalar1=gcol(gidx, kc), scalar2=None, op0=ALU.mult),
                       reads=[xR[kc], cR], writes=[hR[kc]])
                else:
                    OP("act", lambda e, kc=kc: e.activation(out=hT[:, kc, :], in_=xT[:, kc, :], func=AF.Copy, scale=gcol(gidx, kc)),
                       reads=[xR[kc], cR], writes=[hR[kc]])
                OP("act", lambda e, kc=kc: e.activation(out=sq[:, kc, :], in_=xT[:, kc, :], func=AF.Square), reads=[xR[kc]], writes=[sqRl[kc]])

        def norm_stats(token_major):
            bk, br = nbank()
            if token_major:
                for c in range(4):
                    for kc in range(8):
                        OP("pe", lambda e, c=c, kc=kc: e.matmul(bk[:, c:c + 1], lhsT=sq[:, kc, c * 128:(c + 1) * 128], rhs=ones[:, 0:1],
                                                                start=(kc == 0), stop=(kc == 7)),
                           reads=[sqRl[kc], cbR], writes=[br], inc=(c == 3 and kc == 7))
                OP("act", lambda e: e.activation(out=rtok[:, 0:4], in_=bk[:, 0:4], func=AF.Ln, bias=epsT[:, 0:1], scale=1.0 / D),
                   reads=[br, cR], writes=[rtokR])
                OP("act", lambda e: e.activation(out=rtok[:, 4:8], in_=rtok[:, 0:4], func=AF.Exp, scale=-0.5), reads=[rtokR], writes=[rtokR])
                for c in range(4):
                    OP("dve", lambda e, c=c: e.tensor_scalar(out=rtok[:, 8 + 4 * c:12 + 4 * c], in0=dq, scalar1=rtok[:, 4 + c:5 + c], scalar2=None, op0=ALU.mult),
                       reads=[rtokR, cR], writes=[rtokQ[c]])
                for c in range(4):
                    OP("dve", lambda e, c=c: e.tensor_scalar(out=rtok[:, 24 + 4 * c:28 + 4 * c], in0=dk, scalar1=rtok[:, 4 + c:5 + c], scalar2=None, op0=ALU.mult),
                       reads=[rtokR, cR], writes=[rtokK[c]])
            else:
                mm_group(bk[:], br, [(ones, sq[:, kc, :]) for kc in range(8)], sqRl + [cbR])
                OP("act", lambda e: e.activation(out=rt[:], in_=bk[:], func=AF.Ln, bias=epsT[:, 0:1], scale=1.0 / D), reads=[br, cR], writes=[rtR])
                OP("act", lambda e: e.activation(out=rstd[:], in_=rt[:], func=AF.Exp, scale=-0.5), reads=[rtR], writes=[rstdR])

        AR.reset()
        memT = AR.alloc(8 * NMEM, F32)
        memn = AR.alloc(8 * NMEM, BF16)
        memT3 = memT.ap.rearrange("p (k m) -> p k m", k=8)
        memn3 = memn.ap.rearrange("p (k m) -> p k m", k=8)
        S_.dma("sp", d_mem, memT3, memT_d.rearrange("(k p) m -> p k m", p=128), writes=memT.res)
        memTR = memT.res[0:8]
        memnR = [memn.res[k // 2] for k in range(8)]
        norm(memT3, memTR, 3 * NL + 1, memn3, memnR, NMEM)
        for l in range(NL):
            for p in range(4):
                sl, sr = w_get()
                sl3 = sl[:].rearrange("p (k n) -> p k n", k=8)
                if p < 2:
                    for dd in range(4):
                        dch = p * 4 + dd
                        bk, br = nbank()
                        mm_group(bk[:, 0:NMEM], br,
                                 [(sl3[:, kc, dd * 128:(dd + 1) * 128], memn3[:, kc, :]) for kc in range(8)],
                                 [sr] + memnR)
                        OP("act" if dd % 2 == 0 else "dve",
                           (lambda e, bk=bk, dch=dch: e.activation(out=KT[:, l * 8 + dch, :], in_=bk[:, 0:NMEM], func=AF.Copy)) if dd % 2 == 0 else
                           (lambda e, bk=bk, dch=dch: e.tensor_copy(out=KT[:, l * 8 + dch, :], in_=bk[:, 0:NMEM])),
                           reads=[br], writes=[kvR[l]])
                else:
                    for mch in range(2):
                        bk, br = nbank()
                        mm_group(bk[:], br,
                                 [(memn3[:, kc, mch * 128:(mch + 1) * 128], sl3[:, kc, :]) for kc in range(8)],
                                 [sr] + memnR)
                        OP("act" if mch == 0 else "dve",
                           (lambda e, bk=bk, mch=mch: e.activation(out=Vm[:, l * 2 + mch, (p - 2) * 512:(p - 1) * 512], in_=bk[:], func=AF.Copy)) if mch == 0 else
                           (lambda e, bk=bk, mch=mch: e.tensor_copy(out=Vm[:, l * 2 + mch, (p - 2) * 512:(p - 1) * 512], in_=bk[:])),
                           reads=[br], writes=[kvR[l]])
                w_prefetch()

        def sin_reduced(out_ap, ang, tmp1, tmp2i, allres):
            OP("dve", lambda e: e.tensor_scalar(out=tmp1, in0=ang, scalar1=1.0 / (2 * math.pi), scalar2=None, op0=ALU.mult),
               reads=allres, writes=allres)
            OP("dve", lambda e: e.tensor_copy(out=tmp2i, in_=tmp1), reads=allres, writes=allres)
            OP("dve", lambda e: e.tensor_copy(out=tmp1, in_=tmp2i), reads=allres, writes=allres)
            OP("dve", lambda e: e.scalar_tensor_tensor(out=ang, in0=tmp1, scalar=-2 * math.pi, in1=ang, op0=ALU.mult, op1=ALU.add),
               reads=allres, writes=allres)
            OP("dve", lambda e: e.tensor_scalar(out=tmp1, in0=ang, scalar1=math.pi, scalar2=-2 * math.pi, op0=ALU.is_gt, op1=ALU.mult),
               reads=allres, writes=allres)
            OP("dve", lambda e: e.tensor_tensor(out=ang, in0=ang, in1=tmp1, op=ALU.add), reads=allres, writes=allres)
            OP("dve", lambda e: e.tensor_scalar(out=tmp1, in0=ang, scalar1=-math.pi, scalar2=2 * math.pi, op0=ALU.is_lt, op1=ALU.mult),
               reads=allres, writes=allres)
            OP("dve", lambda e: e.tensor_tensor(out=ang, in0=ang, in1=tmp1, op=ALU.add), reads=allres, writes=allres)
            OP("dve", lambda e: e.tensor_scalar(out=ang, in0=ang, scalar1=math.pi, scalar2=-math.pi, op0=ALU.min, op1=ALU.max),
               reads=allres, writes=allres)
            OP("act", lambda e: e.activation(out=out_ap, in_=ang, func=AF.Sin), reads=allres, writes=[ropeR])

        for t in range(NT):
            gc0 = t * 4
            S_.dma("sp", d_x, xT[:], xT_d.rearrange("(k p) s -> p k s", p=128)[:, :, t * TT:(t + 1) * TT], writes=xR)

            AR.reset()
            ang = AR.alloc(256, F32)
            ang2 = AR.alloc(256, F32)
            tmp1 = AR.alloc(256, F32)
            tmp2 = AR.alloc(256, F32)
            allres = ang.res + ang2.res + tmp1.res + tmp2.res
            tmp2i = tmp2.ap.bitcast(I32)
            for c in range(4):
                OP("dve", lambda e, c=c: e.tensor_scalar(out=ang.ap[:, c * 64:(c + 1) * 64], in0=invf, scalar1=posf[:, gc0 + c:gc0 + c + 1],
                                                          scalar2=None, op0=ALU.mult), reads=[cR] + allres, writes=allres)
            OP("dve", lambda e: e.tensor_scalar(out=ang2.ap, in0=ang.ap, scalar1=math.pi / 2, scalar2=None, op0=ALU.add),
               reads=allres, writes=allres)
            sin_reduced(sinT[:].rearrange("p c f -> p (c f)"), ang.ap, tmp1.ap, tmp2i, allres)
            sin_reduced(cosT[:].rearrange("p c f -> p (c f)"), ang2.ap, tmp1.ap, tmp2i, allres)

            for l in range(NL):
                S_.label = 'M.norm'
                S_.dma("sp", d_gn, gn[:], gn_d[l], writes=[gnR])
                S_.dma("pool", d_pw[l % 2], pwb[l % 2][:], pw_d[l], writes=[pwR[l % 2]])
                norm_h(3 * l + 0)
                AR.reset()
                qT_ = [AR.alloc(512, BF16) for _ in range(4)]
                kT_ = [AR.alloc(512, BF16) for _ in range(4)]
                kt_ = [AR.alloc(512, BF16) for _ in range(4)]
                v_ = [AR.alloc(512, BF16) for _ in range(4)]
                u_ = [AR.alloc(512, BF16) for _ in range(4)]
                sg_ = [AR.alloc(512, BF16) for _ in range(4)]
                qt_ = [AR.alloc(512, BF16) for _ in range(3)]
                qs_ = [AR.alloc(512, F32) for _ in range(2)]
                tq = [AR.alloc(256, F32) for _ in range(2)]
                tk = [AR.alloc(256, F32) for _ in range(2)]
                pT_ = [AR.alloc(512, BF16) for _ in range(4)]
                Sb_ = [AR.alloc(512, BF16) for _ in range(4)]
                yr_ = [AR.alloc(512, BF16) for _ in range(2)]
                pl_ = [AR.alloc(512, BF16) for _ in range(4)]
                sgf = qs_
                ss_ = AR.alloc(16, F32)
                rh_ = AR.alloc(32, F32)

                def rope(eng, src, dstbuf, c, tt):
                    s4 = src.ap.rearrange("p (h x f) -> p h x f", h=4, x=2)
                    d4 = dstbuf.ap.rearrange("p (h x f) -> p h x f", h=4, x=2)
                    cs = cosT[:, c, :].unsqueeze(1).to_broadcast([128, 4, 64])
                    sn = sinT[:, c, :].unsqueeze(1).to_broadcast([128, 4, 64])
                    t1 = tt[0].ap.rearrange("p (h f) -> p h f", h=4)
                    t2 = tt[1].ap.rearrange("p (h f) -> p h f", h=4)
                    rr = src.res + [ropeR]
                    tr_ = tt[0].res + tt[1].res
                    OP(eng, lambda e: e.tensor_tensor(out=t1, in0=s4[:, :, 0, :], in1=cs, op=ALU.mult), reads=rr, writes=tt[0].res)
                    OP(eng, lambda e: e.tensor_tensor(out=t2, in0=s4[:, :, 1, :], in1=sn, op=ALU.mult), reads=rr, writes=tt[1].res)
                    OP(eng, lambda e: e.tensor_tensor(out=d4[:, :, 0, :], in0=t1, in1=t2, op=ALU.subtract), reads=tr_, writes=dstbuf.res)
                    OP(eng, lambda e: e.tensor_tensor(out=t1, in0=s4[:, :, 1, :], in1=cs, op=ALU.mult), reads=rr, writes=tt[0].res)
                    OP(eng, lambda e: e.tensor_tensor(out=t2, in0=s4[:, :, 0, :], in1=sn, op=ALU.mult), reads=rr, writes=tt[1].res)
                    OP(eng, lambda e: e.tensor_tensor(out=d4[:, :, 1, :], in0=t1, in1=t2, op=ALU.add), reads=tr_, writes=dstbuf.res)

                S_.label = 'M.P1'
                deferred = []

                def tr_q(c, qt):
                    def f():
                        b2, b2r = nbank()
                        b2b = b2[:].bitcast(BF16)
                        for h in range(4):
                            OP("pe", lambda e, h=h: e.transpose(b2b[:, h * 128:(h + 1) * 128], qt.ap[:, h * 128:(h + 1) * 128], ident),
                               reads=qt.res + [cbR], writes=[b2r], inc=(h == 3))
                        OP("act", lambda e: e.activation(out=qT_[c].ap, in_=b2b[:, 0:512], func=AF.Copy), reads=[b2r], writes=qT_[c].res)
                    return f

                def tr_k(c):
                    def f():
                        b2, b2r = nbank()
                        b2b = b2[:].bitcast(BF16)
                        for h in range(4):
                            OP("pe", lambda e, h=h: e.transpose(b2b[:, h * 128:(h + 1) * 128], kt_[c].ap[:, h * 128:(h + 1) * 128], ident),
                               reads=kt_[c].res + [cbR], writes=[b2r], inc=(h == 3))
                        OP("dve", lambda e: e.tensor_copy(out=kT_[c].ap, in_=b2b[:, 0:512]), reads=[b2r], writes=kT_[c].res)
                    return f

                for j in range(5):
                    sl, sr = w_get()
                    sl3 = sl[:].rearrange("p (k n) -> p k n", k=8)
                    pre = {}
                    if j == 0:
                        for c in range(2):
                            bk, br = nbank()
                            mm_group(bk[:], br, [(hT[:, kc, c * 128:(c + 1) * 128], sl3[:, kc, :]) for kc in range(8)], [sr] + hR)
                            pre[c] = (bk, br)
                        norm_stats(True)
                    for c in range(4):
                        if c in pre:
                            bk, br = pre[c]
                        else:
                            bk, br = nbank()
                            mm_group(bk[:], br, [(hT[:, kc, c * 128:(c + 1) * 128], sl3[:, kc, :]) for kc in range(8)], [sr] + hR)
                        if len(deferred) > (2 if j == 0 else 4):
                            deferred.pop(0)()
                        if j == 0 or j == 1:
                            qs = qs_[(j * 4 + c) % 2]
                            so = (8 if j == 0 else 24) + 4 * c
                            for h in range(4):
                                OP("act", lambda e, h=h, bk=bk, qs=qs, so=so: e.activation(
                                    out=qs.ap[:, h * 128:(h + 1) * 128], in_=bk[:, h * 128:(h + 1) * 128], func=AF.Copy, scale=rtok[:, so + h:so + h + 1]),
                                   reads=[br, (rtokQ if j == 0 else rtokK)[c]], writes=qs.res)
                            if j == 0:
                                qt = qt_[c % 3]
                                rope("dve", qs, qt, c, tq)
                                deferred.append(tr_q(c, qt))
                            else:
                                rope("pool", qs, kt_[c], c, tk)
                                deferred.append(tr_k(c))
                        elif j == 2:
                            OP("act", lambda e, bk=bk, c=c: e.activation(out=v_[c].ap, in_=bk[:], func=AF.Copy, scale=rtok[:, 4 + c:5 + c]),
                               reads=[br, rtokR], writes=v_[c].res)
                        elif j == 3:
                            sf = sgf[c % 2]
                            OP("act", lambda e, bk=bk, sf=sf, c=c: e.activation(out=sf.ap, in_=bk[:], func=AF.Silu, scale=rtok[:, 4 + c:5 + c]),
                               reads=[br, rtokR], writes=sf.res)
                            OP("pool", lambda e, sf=sf, c=c: e.tensor_tensor(out=sg_[c].ap, in0=sf.ap, in1=gn[:], op=ALU.mult),
                               reads=sf.res + [gnR], writes=sg_[c].res)
                        else:
                            OP("dve", lambda e, bk=bk, c=c: e.tensor_scalar(out=u_[c].ap, in0=bk[:], scalar1=rtok[:, 4 + c:5 + c], scalar2=None, op0=ALU.mult),
                               reads=[br, rtokR], writes=u_[c].res)
                    w_prefetch()
                while deferred:
                    deferred.pop(0)()

                S_.label = 'M.P2'
                yT = big
                pw = pwb[l % 2]

                def stA(c):
                    gc = gc0 + c
                    bs, bsr = nbank()
                    for h in range(4):
                        OP("pe", lambda e, h=h: e.matmul(bs[:, h * 128:(h + 1) * 128], lhsT=kT_[c].ap[:, h * 128:(h + 1) * 128],
                                                         rhs=qT_[c].ap[:, h * 128:(h + 1) * 128], start=True, stop=True),
                           reads=kT_[c].res + qT_[c].res, writes=[bsr], inc=(h == 3))
                    pT = pT_[c]
                    OP("dve", lambda e: e.tensor_tensor(out=pT.ap.rearrange("p (h i) -> p h i", h=4), in0=bs[:].rearrange("p (h i) -> p h i", h=4),
                                                        in1=maskf.unsqueeze(1).to_broadcast([128, 4, 128]), op=ALU.mult),
                       reads=[bsr, cR], writes=pT.res)
                    bkv, bkvr = nbank()
                    for h in range(4):
                        OP("pe", lambda e, h=h: e.matmul(bkv[:, h * 128:(h + 1) * 128], lhsT=kt_[c].ap[:, h * 128:(h + 1) * 128],
                                                         rhs=v_[c].ap[:, h * 128:(h + 1) * 128], start=True, stop=True),
                           reads=kt_[c].res + v_[c].res, writes=[bkvr], inc=(h == 3))
                    Sb = Sb_[c]
                    for h in range(4):
                        OP("dve", lambda e, h=h: e.scalar_tensor_tensor(out=Tst[:, l, h * 128:(h + 1) * 128], in0=Tst[:, l, h * 128:(h + 1) * 128],
                                                                        scalar=GAMC[h], in1=bkv[:, h * 128:(h + 1) * 128], op0=ALU.mult, op1=ALU.add),
                           reads=[bkvr, stT[l]], writes=[stT[l]])
                    for h in range(4):
                        OP("pool", lambda e, h=h: e.tensor_scalar(out=Sb.ap[:, h * 128:(h + 1) * 128], in0=Tst[:, l, h * 128:(h + 1) * 128],
                                                                  scalar1=GAMC[h], scalar2=None, op0=ALU.mult),
                           reads=[stT[l]], writes=Sb.res)
                    bp, bpr = nbank()
                    for g in range(4):
                        if gc == 0:
                            OP("pe", lambda e, g=g: e.matmul(bp[:, g * 128:(g + 1) * 128], lhsT=u_[c].ap[:, g * 128:(g + 1) * 128],
                                                             rhs=band0b[:, g * 128:(g + 1) * 128], start=True, stop=True),
                               reads=u_[c].res + [cbR], writes=[bpr], inc=(g == 3))
                        else:
                            if c == 0:
                                up_ap, up_res = uprev[:, l, :], [stU[l]]
                            else:
                                up_ap, up_res = u_[c - 1].ap, u_[c - 1].res
                            OP("pe", lambda e, g=g: e.matmul(bp[:, g * 128:(g + 1) * 128], lhsT=u_[c].ap[:, g * 128:(g + 1) * 128],
                                                             rhs=bandb[:, g * 128:(g + 1) * 128], start=True, stop=False),
                               reads=u_[c].res + [cbR], writes=[bpr], inc=False)
                            OP("pe", lambda e, g=g, up_ap=up_ap: e.matmul(bp[:, g * 128:(g + 1) * 128], lhsT=up_ap[:, g * 128:(g + 1) * 128],
                                                                          rhs=bandpb[:, g * 128:(g + 1) * 128], start=False, stop=True),
                               reads=up_res + [cbR], writes=[bpr], inc=(g == 3))
                    pl = pl_[c]
                    OP("act", lambda e: e.activation(out=pl.ap, in_=bp[:], func=AF.Copy), reads=[bpr], writes=pl.res)

                def stB(c):
                    pT = pT_[c]
                    bo, bor = nbank()
                    if c == 0:
                        Sprev_ap, Sprev_res = SbfL[:, l, :], [stS[l]]
                    else:
                        Sprev_ap, Sprev_res = Sb_[c - 1].ap, Sb_[c - 1].res
                    for h in range(4):
                        OP("pe", lambda e, h=h: e.matmul(bo[:, h * 128:(h + 1) * 128], lhsT=pT.ap[:, h * 128:(h + 1) * 128],
                                                         rhs=v_[c].ap[:, h * 128:(h + 1) * 128], start=True, stop=False),
                           reads=pT.res + v_[c].res, writes=[bor], inc=False)
                        OP("pe", lambda e, h=h: e.matmul(bo[:, h * 128:(h + 1) * 128], lhsT=qT_[c].ap[:, h * 128:(h + 1) * 128],
                                                         rhs=Sprev_ap[:, h * 128:(h + 1) * 128], start=False, stop=True),
                           reads=qT_[c].res + Sprev_res, writes=[bor], inc=(h == 3))
                    for h in range(4):
                        OP("act", lambda e, h=h: e.activation(out=sgf[0].ap[:, h * 128:(h + 1) * 128], in_=bo[:, h * 128:(h + 1) * 128],
                                                              func=AF.Square, accum_out=ss_.ap[:, 4 * c + h:4 * c + h + 1]),
                           reads=[bor], writes=sgf[0].res + ss_.res)
                    OP("act", lambda e: e.activation(out=rh_.ap[:, 8 * c:8 * c + 4], in_=ss_.ap[:, 4 * c:4 * c + 4], func=AF.Ln, bias=epsT[:, 0:1], scale=1.0 / 128),
                       reads=ss_.res + [cR], writes=rh_.res)
                    OP("act", lambda e: e.activation(out=rh_.ap[:, 8 * c + 4:8 * c + 8], in_=rh_.ap[:, 8 * c:8 * c + 4], func=AF.Exp, scale=-0.5),
                       reads=rh_.res, writes=rh_.res)
                    yr = yr_[c % 2]
                    for h in range(4):
                        OP("dve", lambda e, h=h: e.scalar_tensor_tensor(out=yr.ap[:, h * 128:(h + 1) * 128], in0=bo[:, h * 128:(h + 1) * 128],
                                                                        scalar=rh_.ap[:, 8 * c + 4 + h:8 * c + 5 + h], in1=sg_[c].ap[:, h * 128:(h + 1) * 128],
                                                                        op0=ALU.mult, op1=ALU.mult),
                           reads=[bor] + rh_.res + sg_[c].res, writes=yr.res)
                    pl = pl_[c]
                    bq, bqr = nbank()
                    for g in range(4):
                        OP("pe", lambda e, g=g: e.matmul(bq[:, g * 128:(g + 1) * 128], lhsT=pw[:, g * 128:(g + 1) * 128],
                                                         rhs=pl.ap[:, g * 128:(g + 1) * 128], start=True, stop=True),
                           reads=pl.res + [pwR[l % 2]], writes=[bqr], inc=(g == 3))
                    OP("dve", lambda e: e.tensor_tensor(out=yT[:, 4:8, c * 128:(c + 1) * 128],
                                                        in0=bq[:].rearrange("p (g t) -> p g t", g=4),
                                                        in1=psc[:, l * 4:(l + 1) * 4].unsqueeze(2).to_broadcast([128, 4, 128]), op=ALU.mult),
                       reads=[bqr, cR], writes=BRh(4, 8, c // 2))

                def stC(c):
                    yr = yr_[c % 2]
                    by, byr = nbank()
                    byb = by[:].bitcast(BF16)
                    for h in range(4):
                        OP("pe", lambda e, h=h: e.transpose(byb[:, h * 128:(h + 1) * 128], yr.ap[:, h * 128:(h + 1) * 128], ident),
                           reads=yr.res + [cbR], writes=[byr], inc=(h == 3))
                    OP("act", lambda e: e.activation(out=yT[:, 0:4, c * 128:(c + 1) * 128],
                                                     in_=byb[:, 0:512].rearrange("p (h t) -> p h t", h=4), func=AF.Copy),
                       reads=[byr], writes=BRh(0, 4, c // 2))

                wsl = [w_get(), w_get()]
                wsl3 = [(a[:].rearrange("p (k n) -> p k n", k=8), r) for a, r in wsl]

                def p3(dm, half):
                    t0_, t1_ = half * 256, (half + 1) * 256
                    bk, br = nbank()
                    w3, wr = wsl3[dm // 4]
                    mm_group(bk[:, 0:256], br, [(w3[:, k, (dm % 4) * 128:(dm % 4 + 1) * 128], yT[:, k, t0_:t1_]) for k in range(8)],
                             BRh(0, 8, half) + [wr])
                    OP("dve", lambda e: e.tensor_tensor(out=xT[:, dm, t0_:t1_], in0=bk[:, 0:256], in1=xT[:, dm, t0_:t1_], op=ALU.add),
                       reads=[br, xR[dm]], writes=[xR[dm]])

                stA(0); stA(1); stB(0); stA(2); stB(1); stC(0); stA(3); stB(2); stC(1)
                S_.label = 'M.P3'
                p3(0, 0); p3(1, 0); p3(2, 0)
                S_.label = 'M.P2'
                stB(3)
                S_.label = 'M.P3'
                p3(3, 0); p3(4, 0); p3(5, 0)
                S_.label = 'M.P2'
                stC(2)
                S_.label = 'M.P3'
                p3(6, 0); p3(7, 0)
                S_.label = 'M.P2'
                stC(3)
                OP("pool", lambda e: e.tensor_copy(out=SbfL[:, l, :], in_=Sb_[3].ap), reads=Sb_[3].res, writes=[stS[l]])
                OP("pool", lambda e: e.tensor_copy(out=uprev[:, l, :], in_=u_[3].ap), reads=u_[3].res, writes=[stU[l]])

                S_.label = 'M.P3'
                def proj_residual(src, srcR, nk, lhs_of):
                    for dm in range(8):
                        bk, br = nbank()
                        prs, rds = [], list(srcR)
                        for k in range(nk):
                            lh, lr = lhs_of(dm, k)
                            prs.append((lh, src[:, k, :]))
                            if lr not in rds:
                                rds.append(lr)
                        mm_group(bk[:], br, prs, rds)
                        OP("dve", lambda e, dm=dm, bk=bk: e.tensor_tensor(out=xT[:, dm, :], in0=bk[:], in1=xT[:, dm, :], op=ALU.add),
                           reads=[br, xR[dm]], writes=[xR[dm]])

                for dm in range(8):
                    p3(dm, 1)
                w_prefetch()

                S_.label = 'X'
                norm_h(3 * l + 1)
                AR.reset()
                pe_ = [AR.alloc(1024, BF16) for _ in range(3)]
                rd_ = [AR.alloc(512, F32) for _ in range(2)]
                qTb = big
                wsl = [w_get(), w_get()]
                wsl3 = [(a[:].rearrange("p (k n) -> p k n", k=8), r) for a, r in wsl]
                pre = {}
                for dch in range(2):
                    bk, br = nbank()
                    w3, wr = wsl3[dch // 4]
                    mm_group(bk[:], br, [(w3[:, kc, (dch % 4) * 128:(dch % 4 + 1) * 128], hT[:, kc, :]) for kc in range(8)], [wr] + hR)
                    pre[dch] = (bk, br)
                norm_stats(False)
                for dch in range(8):
                    w3, wr = wsl3[dch // 4]
                    if dch in pre:
                        bk, br = pre[dch]
                    else:
                        bk, br = nbank()
                        mm_group(bk[:], br, [(w3[:, kc, (dch % 4) * 128:(dch % 4 + 1) * 128], hT[:, kc, :]) for kc in range(8)], [wr] + hR)
                    OP("dve", lambda e, bk=bk, dch=dch: e.tensor_tensor(out=qTb[:, 8 + dch, :], in0=bk[:], in1=rstd[:], op=ALU.mult),
                       reads=[br, rstdR], writes=[bR[8 + dch]])
                w_prefetch()
                def xS(h):
                    pe = pe_[h % 3]
                    for mch in range(2):
                        bk, br = nbank()
                        mm_group(bk[:], br, [(KT[:, l * 8 + 2 * h + dd, mch * 128:(mch + 1) * 128], qTb[:, 8 + 2 * h + dd, :]) for dd in range(2)],
                                 [kvR[l], bR[8 + 2 * h], bR[9 + 2 * h]])
                        OP("act", lambda e, bk=bk, mch=mch: e.activation(out=pe.ap[:, mch * 512:(mch + 1) * 512], in_=bk[:], func=AF.Exp, scale=1.0 / 16),
                           reads=[br], writes=pe.res)

                def xD(h):
                    pe = pe_[h % 3]
                    rd = rd_[h % 2]
                    bk, br = nbank()
                    mm_group(bk[:], br, [(ones, pe.ap[:, mch * 512:(mch + 1) * 512]) for mch in range(2)], pe.res + [cbR])
                    OP("act", lambda e: e.activation(out=rd.ap, in_=bk[:], func=AF.Ln), reads=[br], writes=rd.res)
                    OP("act", lambda e: e.activation(out=rd.ap, in_=rd.ap, func=AF.Exp, scale=-1.0), reads=rd.res, writes=rd.res)
                    for dd in range(2):
                        bk2, br2 = nbank()
                        mm_group(bk2[:], br2, [(Vm[:, l * 2 + mch, (2 * h + dd) * 128:(2 * h + dd + 1) * 128], pe.ap[:, mch * 512:(mch + 1) * 512]) for mch in range(2)],
                                 pe.res + [kvR[l]])
                        OP("dve", lambda e, bk2=bk2, dd=dd: e.tensor_tensor(out=big[:, 2 * h + dd, :], in0=bk2[:], in1=rd.ap, op=ALU.mult),
                           reads=[br2] + rd.res, writes=BR(2 * h + dd))

                xS(0); xS(1); xD(0); xS(2); xD(1); xS(3); xD(2); xD(3)
                wsl = [w_get(), w_get()]
                wsl3 = [(a[:].rearrange("p (k n) -> p k n", k=8), r) for a, r in wsl]
                proj_residual(big, BR(0, 8), 8, lambda dm, k: (wsl3[dm // 4][0][:, k, (dm % 4) * 128:(dm % 4 + 1) * 128], wsl3[dm // 4][1]))
                w_prefetch()

                S_.label = 'F.norm'
                norm(xT, xR, 3 * l + 2, hT, hR, TT)
                AR.reset()
                S_.label = 'F.up'
                ab_ = [AR.alloc(TT + 2, F32) for _ in range(8)]
                cc_ = [AR.alloc(TT, F32) for _ in range(8)]
                pend = []
                sf_ = [AR.alloc(TT, F32) for _ in range(3)]
                cwl = lambda tap, ch: cw[:, (l * 4 + tap) * 44 + ch:(l * 4 + tap) * 44 + ch + 1]
                it = 0
                chs = [2 * j + e_ + 22 * gv for j in range(11) for e_ in range(2) for gv in range(2)]

                def tail_in(i):
                    if i < len(chs):
                        ch_, ab_i = chs[i], ab_[i % 8]
                        OP("pool", lambda e: e.tensor_copy(out=ab_i.ap[:, 0:2], in_=atail[:, l * 44 + ch_, :]), reads=[tailR[l][ch_]], writes=ab_i.res)

                tail_in(0)
                tail_in(1)
                for j in range(11):
                    sl, sr = w_get()
                    sl3 = sl[:].rearrange("p (k n) -> p k n", k=8)
                    for e_ in range(2):
                        fc = 2 * j + e_
                        cbufs = []
                        tail_in(it + 2)
                        tail_in(it + 3)
                        for gv in range(2):
                            ch = fc + 22 * gv
                            col0 = gv * 256 + e_ * 128
                            bk, br = nbank()
                            mm_group(bk[:], br, [(sl3[:, kc, col0:col0 + 128], hT[:, kc, :]) for kc in range(8)], [sr] + hR)
                            ab = ab_[it % 8]
                            cc = cc_[it % 8]
                            it += 1
                            tr = tailR[l][ch]
                            assert chs[it - 1] == ch
                            OP("act", lambda e, ab=ab, bk=bk: e.activation(out=ab.ap[:, 2:TT + 2], in_=bk[:], func=AF.Copy), reads=[br], writes=ab.res)
                            OP("act", lambda e, cc=cc, bk=bk, ch=ch: e.activation(out=cc.ap, in_=bk[:], func=AF.Identity, scale=cwl(2, ch), bias=cwl(3, ch)),
                               reads=[br, cR], writes=cc.res)
                            OP("pool", lambda e, ab=ab, ch=ch: e.tensor_copy(out=atail[:, l * 44 + ch, :], in_=ab.ap[:, TT:TT + 2]), reads=ab.res, writes=[tr])
                            OP("dve", lambda e, ab=ab, cc=cc, ch=ch: e.scalar_tensor_tensor(out=cc.ap, in0=ab.ap[:, 1:TT + 1], scalar=cwl(1, ch), in1=cc.ap,
                                                                                            op0=ALU.mult, op1=ALU.add), reads=ab.res + cc.res + [cR], writes=cc.res)
                            OP("dve", lambda e, ab=ab, cc=cc, ch=ch: e.scalar_tensor_tensor(out=cc.ap, in0=ab.ap[:, 0:TT], scalar=cwl(0, ch), in1=cc.ap,
                                                                                            op0=ALU.mult, op1=ALU.add), reads=ab.res + cc.res + [cR], writes=cc.res)
                            cbufs.append(cc)
                        def fin(fc=fc, cg=cbufs[0], cv=cbufs[1]):
                            sf = sf_[fc % 3]
                            OP("act", lambda e: e.activation(out=sf.ap, in_=cg.ap, func=AF.Silu), reads=cg.res, writes=sf.res)
                            OP("pool", lambda e: e.tensor_tensor(out=big[:, fc, :], in0=sf.ap, in1=cv.ap, op=ALU.mult),
                               reads=sf.res + cv.res, writes=BR(fc))
                        pend.append(fin)
                        if len(pend) > 2:
                            pend.pop(0)()
                    w_prefetch()
                while pend:
                    pend.pop(0)()
                S_.label = 'F.down'
                for half, (f0, f1) in enumerate(((0, 12), (12, NFC))):
                    W_ = (f1 - f0) * 128
                    for dmp in range(4):
                        sl, sr = w_get()
                        for dd in range(2):
                            dm = 2 * dmp + dd
                            bk, br = nbank()
                            mm_group(bk[:], br, [(sl[:, dd * W_ + (fc - f0) * 128:dd * W_ + (fc - f0 + 1) * 128], big[:, fc, :]) for fc in range(f0, f1)],
                                     [sr] + BR(f0, f1))
                            OP("dve", lambda e, dm=dm, bk=bk: e.tensor_tensor(out=xT[:, dm, :], in0=bk[:], in1=xT[:, dm, :], op=ALU.add),
                               reads=[br, xR[dm]], writes=[xR[dm]])
                        w_prefetch()

            S_.label = 'final'
            if final_norm:
                for kc in range(8):
                    OP("act", lambda e, kc=kc: e.activation(out=sq[:, kc, :], in_=xT[:, kc, :], func=AF.Square), reads=[xR[kc]], writes=[sqRl[kc]])
                bk, br = nbank()
                mm_group(bk[:], br, [(ones, sq[:, kc, :]) for kc in range(8)], sqRl + [cbR])
                OP("act", lambda e: e.activation(out=rt[:], in_=bk[:], func=AF.Ln, bias=epsT[:, 0:1], scale=1.0 / D), reads=[br, cR], writes=[rtR])
                OP("act", lambda e: e.activation(out=rstd[:], in_=rt[:], func=AF.Exp, scale=-0.5), reads=[rtR], writes=[rstdR])
                for kc in range(8):
                    OP("dve",
                       lambda e, kc=kc: e.scalar_tensor_tensor(out=xT[:, kc, :], in0=xT[:, kc, :], scalar=gcol(3 * NL, kc), in1=rstd[:],
                                                               op0=ALU.mult, op1=ALU.mult), reads=[xR[kc], rstdR, cR], writes=[xR[kc]])
            S_.dma("sp", d_out, out_d.rearrange("(k p) s -> p k s", p=128)[:, :, t * TT:(t + 1) * TT], xT[:], reads=xR)

        de = S_.engs[d_out]
        S_.engs["sp"].handle.wait_ge(de.sem, de.cnt * 16)
        build.stats = (S_.ninstr, S_.nwaits)
        build.labels = S_.labels
    return nc


def _slot_layout(w):
    K, N = w.shape
    n = N // 512
    a = w.reshape(8, 128, n, 512).transpose(2, 1, 0, 3)
    return np.ascontiguousarray(a.reshape(n, 128, 4096))


def prep_shared(NL, mix_norm_g, w_in, ret_gn_g, pool_w, pool_scale, w_out, xa_norm_g, mem_norm_g, xa_wq, xa_wkv, xa_wo,
                ffn_norm_g, ffn_w_up, ffn_conv_w, ffn_conv_b, ffn_w_down, final_norm_g):
    f = lambda a: np.asarray(a, np.float32)
    cblob, _ = _consts()
    gv = []
    for l in range(NL):
        for g in (mix_norm_g, xa_norm_g, ffn_norm_g):
            gv.append(f(g)[l].reshape(8, 128).T)
    gv.append(f(final_norm_g).reshape(8, 128).T)
    gv.append(f(mem_norm_g).reshape(8, 128).T)
    gvec = np.ascontiguousarray(np.concatenate(gv, 1))
    gn = np.ascontiguousarray(np.broadcast_to(f(ret_gn_g)[:NL, None, :], (NL, 128, 512)))
    psc = np.ascontiguousarray(np.concatenate([f(pool_scale)[l].reshape(4, 128).T for l in range(NL)], 1))
    cws = []
    for l in range(NL):
        for tap in range(3):
            cws.append(f(ffn_conv_w)[l, tap].reshape(44, 128).T)
        cws.append(f(ffn_conv_b)[l].reshape(44, 128).T)
    cw = np.ascontiguousarray(np.concatenate(cws, 1))
    w_in_r = np.stack([_slot_layout(f(w_in)[l]) for l in range(NL)])
    w_out_r = np.stack([_slot_layout(f(w_out)[l]) for l in range(NL)])
    wq_r = np.stack([_slot_layout(f(xa_wq)[l]) for l in range(NL)])
    wo_r = np.stack([_slot_layout(f(xa_wo)[l]) for l in range(NL)])
    wkv_r = np.stack([_slot_layout(f(xa_wkv)[l]) for l in range(NL)])
    ups = []
    for l in range(NL):
        wu = f(ffn_w_up)[l]
        cols = []
        for j in range(11):
            cols.append(wu[:, 256 * j:256 * j + 256])
            cols.append(wu[:, DFF + 256 * j:DFF + 256 * j + 256])
        ups.append(_slot_layout(np.concatenate(cols, 1)))
    w_up_r = np.stack(ups)
    dn = []
    for l in range(NL):
        a = f(ffn_w_down)[l].reshape(NFC, 128, 8, 128).transpose(2, 1, 0, 3)
        pa = a[:, :, 0:12, :].reshape(4, 2, 128, 1536).transpose(0, 2, 1, 3).reshape(4, 128, 3072)
        pb = np.zeros((4, 128, 3072), np.float32)
        pb[:, :, 0:2560] = a[:, :, 12:NFC, :].reshape(4, 2, 128, 1280).transpose(0, 2, 1, 3).reshape(4, 128, 2560)
        dn.append(np.concatenate([pa, pb], 0))
    w_dn_r = np.stack(dn)
    pw_r = np.stack([np.ascontiguousarray(f(pool_w)[l].transpose(1, 0, 2)).reshape(128, 512) for l in range(NL)])
    return {"consts": cblob, "gvec": gvec, "gn": gn, "psc": psc, "cw": cw, "w_in": w_in_r, "w_out": w_out_r, "wq": wq_r,
            "wo": wo_r, "wkv": wkv_r, "w_up": w_up_r, "w_dn": w_dn_r, "pw": pw_r}


def prep_core(x_b, mem_b, pos_b):
    S = x_b.shape[0]
    return {"xT": np.ascontiguousarray(np.asarray(x_b, np.float32).T),
            "memT": np.ascontiguousarray(np.asarray(mem_b, np.float32).T),
            "pos": np.ascontiguousarray(np.asarray(pos_b, np.int32).reshape(S // 128, 128).T)}


_NC_CACHE = {}


def run_cores(x, mem, positions, params, NL, final_norm=True):
    B, S, _ = x.shape
    key = (S, NL, final_norm)
    if key not in _NC_CACHE:
        _NC_CACHE[key] = build(S, NL, final_norm)
    nc = _NC_CACHE[key]
    shared = prep_shared(NL, **params)
    in_maps = []
    for b in range(B):
        m = dict(shared)
        m.update(prep_core(x[b], mem[b], positions[b]))
        in_maps.append(m)
    res = run_bass_kernel_spmd(nc, in_maps, core_ids=list(range(B)))
    return np.stack([np.ascontiguousarray(r["out"].T) for r in res.results]).astype(np.float32)


def kernel(x, mem, positions, mix_norm_g, w_in, ret_gn_g, pool_w, pool_scale, w_out, xa_norm_g, mem_norm_g, xa_wq,
           xa_wkv, xa_wo, ffn_norm_g, ffn_w_up, ffn_conv_w, ffn_conv_b, ffn_w_down, final_norm_g):
    params = dict(mix_norm_g=mix_norm_g, w_in=w_in, ret_gn_g=ret_gn_g, pool_w=pool_w, pool_scale=pool_scale, w_out=w_out,
                  xa_norm_g=xa_norm_g, mem_norm_g=mem_norm_g, xa_wq=xa_wq, xa_wkv=xa_wkv, xa_wo=xa_wo, ffn_norm_g=ffn_norm_g,
                  ffn_w_up=ffn_w_up, ffn_conv_w=ffn_conv_w, ffn_conv_b=ffn_conv_b, ffn_w_down=ffn_w_down,
                  final_norm_g=final_norm_g)
    x = np.asarray(x, np.float32)
    mem = np.asarray(mem, np.float32)
    positions = np.asarray(positions)
    NL = np.asarray(w_in).shape[0]
    if FUSED:
        return run_cores(x, mem, positions, params, NL, True)
    per_layer = ("mix_norm_g", "w_in", "ret_gn_g", "pool_w", "pool_scale", "w_out", "xa_norm_g", "xa_wq", "xa_wkv", "xa_wo",
                 "ffn_norm_g", "ffn_w_up", "ffn_conv_w", "ffn_conv_b", "ffn_w_down")
    cur = x
    for l in range(NL):
        pl = {k: (np.asarray(v)[l:l + 1] if k in per_layer else v) for k, v in params.items()}
        cur = run_cores(cur, mem, positions, pl, 1, l == NL - 1)
    return cur
```

```python
import math
from contextlib import ExitStack

import numpy as np
import concourse.bass as bass
import concourse.mybir as mybir
from concourse.bass_utils import run_bass_kernel_spmd

F32 = mybir.dt.float32
BF16 = mybir.dt.bfloat16
I32 = mybir.dt.int32
ALU = mybir.AluOpType
AF = mybir.ActivationFunctionType

D = 1024
NMEM = 256
DFF = 2816
NFC = 22
EPS = 1e-6
TT = 512
FUSED = True
LABELS = False
NSLOT = 5
SLOTW = 4096
WINDOWS = (2, 4, 8, 16)
GAM = [1.0 - 2.0 ** (-5 - h) for h in range(4)]
GAMC = [float(np.float32(np.exp(np.float32(np.log1p(-np.float32(2.0 ** (-5 - h)))) * 128))) for h in range(4)]


class Res:
    __slots__ = ("name", "w", "rd")

    def __init__(self, name):
        self.name = name
        self.w = None
        self.rd = {}


class _Eng:
    def __init__(self, name, handle, sem, step):
        self.name = name
        self.handle = handle
        self.sem = sem
        self.step = step
        self.cnt = 0
        self.seen = {}
        self.hist = {}


class Sync:
    def __init__(self, nc, stack):
        self.nc = nc
        self.stack = stack
        self.engs = {}
        self.nwaits = 0
        self.ninstr = 0
        self.label = ""
        self.labels = {}
        for name, h in (("pe", nc.tensor), ("act", nc.scalar), ("dve", nc.vector),
                        ("pool", nc.gpsimd), ("sp", nc.sync)):
            sem = stack.enter_context(nc.semaphore("s_" + name))
            self.engs[name] = _Eng(name, h, sem, 1)

    def dma_stream(self, name):
        sem = self.stack.enter_context(self.nc.semaphore("d_" + name))
        e = _Eng("dma:" + name, None, sem, 16)
        self.engs[e.name] = e
        return e.name

    def _deps(self, eng, reads, writes):
        deps = {}

        def add(p, v, raw):
            if p == eng and (eng == "pe" or not raw):
                return
            if deps.get(p, 0) < v:
                deps[p] = v

        for r in reads:
            if r.w is not None:
                add(r.w[0], r.w[1], True)
        for r in writes:
            if r.w is not None:
                add(r.w[0], r.w[1], False)
            for p, v in r.rd.items():
                add(p, v, False)
        return deps

    def _wait(self, e, deps):
        for p, v in deps.items():
            if e.seen.get(p, 0) >= v:
                continue
            pe = self.engs[p]
            e.handle.wait_ge(pe.sem, v)
            self.nwaits += 1
            snap = pe.hist.get(v)
            if snap is None:
                snap = pe.seen
            for q, u in snap.items():
                if e.seen.get(q, 0) < u:
                    e.seen[q] = u
            e.seen[p] = v

    def op(self, eng, emit, reads=(), writes=(), inc=True):
        e = self.engs[eng]
        self._wait(e, self._deps(eng, reads, writes))
        ins = emit(e.handle)
        self.ninstr += 1
        if LABELS:
            self.labels[ins.ins.name] = self.label
        val = (e.cnt + 1) * e.step
        if inc:
            ins.then_inc(e.sem, e.step)
            e.cnt += 1
            e.hist[val] = dict(e.seen)
        for r in reads:
            if r.rd.get(eng, 0) < val:
                r.rd[eng] = val
        for r in writes:
            r.w = (eng, val)
            r.rd = {}
        return ins

    def dma(self, queue, stream, out, in_, reads=(), writes=()):
        q = self.engs[queue]
        d = self.engs[stream]
        self._wait(q, self._deps(stream, reads, writes))
        ins = q.handle.dma_start(out=out, in_=in_)
        ins.then_inc(d.sem, 16)
        self.ninstr += 1
        d.cnt += 1
        val = d.cnt * 16
        d.hist[val] = dict(q.seen)
        d.seen = dict(q.seen)
        for r in reads:
            if r.rd.get(stream, 0) < val:
                r.rd[stream] = val
        for r in writes:
            r.w = (stream, val)
            r.rd = {}
        return ins


def _consts():
    cols = {}
    parts = []
    off = 0

    def add(name, arr):
        nonlocal off
        arr = np.asarray(arr, np.float32)
        assert arr.shape[0] == 128
        cols[name] = (off, arr.shape[1])
        parts.append(arr)
        off += arr.shape[1]

    idx = np.arange(128)
    add("ident", np.eye(128))
    add("ones", np.ones((128, 128)))
    m = (idx[:, None] <= idx[None, :]).astype(np.float32)
    add("mask", m)
    band, bandp, band0 = [], [], []
    for w in WINDOWS:
        t = idx[:, None]
        j = idx[None, :]
        B = ((j >= t - w + 1) & (j <= t)).astype(np.float64) / w - (j == t)
        Bp = ((j - 128 >= t - w + 1) & (j - 128 <= -1)).astype(np.float64) / w
        cnt = np.minimum(t + 1, w)
        B0 = ((j >= np.maximum(t - w + 1, 0)) & (j <= t)).astype(np.float64) / cnt - (j == t)
        band.append(B.T)
        bandp.append(Bp.T)
        band0.append(B0.T)
    add("band", np.concatenate(band, 1))
    add("bandp", np.concatenate(bandp, 1))
    add("band0", np.concatenate(band0, 1))
    lg = np.log1p(-np.exp2(-5.0 - np.arange(4, dtype=np.float64)))
    i1 = (idx + 1.0)[:, None]
    add("dq", np.exp(lg[None, :] * i1))
    add("dk", np.exp(-lg[None, :] * i1) * (128.0 ** -0.5))
    invf = np.exp(-math.log(10000.0) * np.arange(0, 128, 2, dtype=np.float32) / 128).astype(np.float32)
    add("invf", np.tile(invf[None, :], (128, 1)))
    return np.concatenate(parts, 1).astype(np.float32), cols


def build(S, NL, final_norm=True):
    assert S % TT == 0
    NT = S // TT
    NCH = S // 128
    cblob, ccols = _consts()
    NC_ = cblob.shape[1]

    nc = bass.Bass("TRN2", target_bir_lowering=False)
    dr = lambda name, shape, dt=F32, kind="ExternalInput": nc.dram_tensor(name, shape, dt, kind=kind).ap()
    xT_d = dr("xT", [D, S])
    memT_d = dr("memT", [D, NMEM])
    pos_d = dr("pos", [128, NCH], I32)
    const_d = dr("consts", [128, NC_])
    gvec_d = dr("gvec", [128, (3 * NL + 2) * 8])
    gn_d = dr("gn", [NL, 128, 512])
    psc_d = dr("psc", [128, NL * 4])
    cw_d = dr("cw", [128, NL * 4 * 44])
    w_in_d = dr("w_in", [NL, 5, 128, SLOTW])
    w_out_d = dr("w_out", [NL, 2, 128, SLOTW])
    wq_d = dr("wq", [NL, 2, 128, SLOTW])
    wo_d = dr("wo", [NL, 2, 128, SLOTW])
    wkv_d = dr("wkv", [NL, 4, 128, SLOTW])
    w_up_d = dr("w_up", [NL, 11, 128, SLOTW])
    w_dn_d = dr("w_dn", [NL, 8, 128, 3072])
    pw_d = dr("pw", [NL, 128, 512])
    out_d = dr("out", [D, S], kind="ExternalOutput")

    with ExitStack() as st:
        S_ = Sync(nc, st)
        sbuf = lambda name, shape, dt: st.enter_context(nc.sbuf_tensor(name, shape, dt))
        psum = lambda name, shape, dt: st.enter_context(nc.psum_tensor(name, shape, dt))

        xT = sbuf("xT_sb", [128, 8, TT], F32)
        xR = [Res(f"xT{k}") for k in range(8)]
        hT = sbuf("hT", [128, 8, TT], BF16)
        hR = [Res(f"hT{k}") for k in range(8)]
        big = sbuf("big", [128, NFC, TT], BF16)
        bR = [Res(f"big{k}") for k in range(NFC)]
        bR2 = [Res(f"bigh{k}") for k in range(8)]

        def BR(k0, k1=None):
            ks = range(k0, k1) if k1 is not None else [k0]
            out = []
            for k in ks:
                out.append(bR[k])
                if k < 8:
                    out.append(bR2[k])
            return out

        def BRh(k0, k1, half):
            return [(bR if half == 0 else bR2)[k] for k in range(k0, k1)]
        KT = sbuf("KT", [128, NL * 8, NMEM], BF16)
        Vm = sbuf("Vm", [128, NL * 2, D], BF16)
        kvR = [Res(f"kv{l}") for l in range(NL)]
        Tst = sbuf("Tst", [128, NL, 512], F32)
        SbfL = sbuf("SbfL", [128, NL, 512], BF16)
        uprev = sbuf("uprev", [128, NL, 512], BF16)
        atail = sbuf("atail", [128, NL * 44, 2], F32)
        stT = [Res(f"stT{l}") for l in range(NL)]
        stS = [Res(f"stS{l}") for l in range(NL)]
        stU = [Res(f"stU{l}") for l in range(NL)]
        tailR = [[Res(f"tail{l}_{c}") for c in range(44)] for l in range(NL)]
        NCF = 128 + 4 + 4 + 64
        cst = sbuf("cst", [128, NCF], F32)
        cR = Res("cst")
        cbf = sbuf("cbf", [128, 256 + 1536], BF16)
        gvec = sbuf("gvec_sb", [128, (3 * NL + 2) * 8], F32)
        psc = sbuf("psc_sb", [128, NL * 4], F32)
        cw = sbuf("cw_sb", [128, NL * 4 * 44], F32)
        gn = sbuf("gn_sb", [128, 512], F32)
        gnR = Res("gn")
        epsT = sbuf("epsT", [128, 1], F32)
        posi = sbuf("posi", [128, NCH], I32)
        posf = sbuf("posf", [128, NCH], F32)
        cosT = sbuf("cosT", [128, 4, 64], F32)
        sinT = sbuf("sinT", [128, 4, 64], F32)
        ropeR = Res("rope")
        slots = [sbuf(f"slot{i}", [128, SLOTW], BF16) for i in range(NSLOT)]
        slotR = [Res(f"slot{i}") for i in range(NSLOT)]
        pwb = [sbuf(f"pwb{i}", [128, 512], BF16) for i in range(2)]
        pwR = [Res(f"pwb{i}") for i in range(2)]

        ident = cbf[:, 0:128]
        ones = cbf[:, 128:256]
        bandb = cbf[:, 256:768]
        bandpb = cbf[:, 768:1280]
        band0b = cbf[:, 1280:1792]
        maskf = cst[:, 0:128]
        dq = cst[:, 128:132]
        dk = cst[:, 132:136]
        invf = cst[:, 136:200]

        rt = sbuf("rt", [128, TT], F32)
        rstd = sbuf("rstd", [128, TT], F32)
        rtR, rstdR = Res("rt"), Res("rstd")
        ARENA_KB = 52
        arena = sbuf("arena", [128, ARENA_KB * 512], BF16)
        aR = [Res(f"ar{i}") for i in range(ARENA_KB)]

        class Buf:
            def __init__(self, ap, res):
                self.ap = ap
                self.res = res

        class Arena:
            def __init__(self):
                self.off = 0

            def reset(self):
                self.off = 0

            def alloc(self, ncols, dt):
                nb = ncols * (4 if dt == F32 else 2)
                nb = (nb + 1023) // 1024 * 1024
                o = self.off
                self.off += nb
                assert self.off <= ARENA_KB * 1024, "arena overflow"
                ap = arena[:, o // 2:(o + nb) // 2]
                if dt == F32:
                    ap = ap.bitcast(F32)
                ap = ap[:, 0:ncols]
                res = aR[o // 1024:(o + nb - 1) // 1024 + 1]
                return Buf(ap, list(res))

        AR = Arena()
        sq = arena[:, (ARENA_KB - 8) * 512:ARENA_KB * 512].rearrange("p (k t) -> p k t", k=8)
        sqRl = aR[ARENA_KB - 8:ARENA_KB]

        banks = [psum(f"bank{i}", [128, 512], F32) for i in range(8)]
        bankR = [Res(f"bank{i}") for i in range(8)]
        bank_i = [0]

        def nbank():
            i = bank_i[0] % 8
            bank_i[0] += 1
            return banks[i], bankR[i]

        d_in = S_.dma_stream("in")
        d_mem = S_.dma_stream("mem")
        d_cb = S_.dma_stream("cb")
        d_x = S_.dma_stream("x")
        d_out = S_.dma_stream("out")
        d_slot = [S_.dma_stream(f"sl{i}") for i in range(NSLOT)]
        d_pw = [S_.dma_stream(f"pw{i}") for i in range(2)]
        d_gn = S_.dma_stream("gn")

        OP = S_.op

        wq_list = []
        w_next = [0]
        w_use = [0]

        def w_issue():
            i = w_next[0]
            if i >= len(wq_list):
                return False
            src, ncols = wq_list[i]
            s = i % NSLOT
            S_.dma("pool", d_slot[s], slots[s][:, 0:ncols], src, writes=[slotR[s]])
            w_next[0] += 1
            return True

        def w_get():
            i = w_use[0]
            while w_next[0] <= i:
                assert w_issue()
            w_use[0] += 1
            s = i % NSLOT
            return slots[s], slotR[s]

        def w_prefetch():
            while w_next[0] < len(wq_list) and w_next[0] < w_use[0] + NSLOT:
                w_issue()

        o0 = ccols["mask"][0]
        S_.dma("sp", d_in, cst[:, 0:128], const_d[:, o0:o0 + 128], writes=[cR])
        o0 = ccols["dq"][0]
        S_.dma("sp", d_in, cst[:, 128:200], const_d[:, o0:o0 + 72], writes=[cR])
        S_.dma("sp", d_in, gvec[:], gvec_d, writes=[cR])
        S_.dma("sp", d_in, psc[:], psc_d, writes=[cR])
        S_.dma("sp", d_in, cw[:], cw_d, writes=[cR])
        S_.dma("sp", d_in, posi[:], pos_d, writes=[cR])
        cbR = Res("cbf")
        OP("dve", lambda e: e.memset(epsT[:], EPS), writes=[cR])
        o0 = ccols["ident"][0]
        S_.dma("pool", d_cb, cbf[:, 0:256], const_d[:, o0:o0 + 256], writes=[cbR])
        o0 = ccols["band"][0]
        S_.dma("pool", d_cb, cbf[:, 256:1792], const_d[:, o0:o0 + 1536], writes=[cbR])
        OP("dve", lambda e: e.tensor_copy(out=posf[:], in_=posi[:]), reads=[cR], writes=[cR])
        for l in range(NL):
            OP("dve", lambda e, l=l: e.memset(Tst[:, l, :], 0.0), writes=[stT[l]])
            OP("pool", lambda e, l=l: e.memset(SbfL[:, l, :], 0.0), writes=[stS[l]])
            OP("pool", lambda e, l=l: e.memset(uprev[:, l, :], 0.0), writes=[stU[l]])
        OP("dve", lambda e: e.memset(atail[:], 0.0), writes=[r for l in range(NL) for r in tailR[l]])

        for l in range(NL):
            for p in range(4):
                wq_list.append((wkv_d[l, p], SLOTW))
        for t in range(NT):
            for l in range(NL):
                for p in range(5):
                    wq_list.append((w_in_d[l, p], SLOTW))
                for p in range(2):
                    wq_list.append((w_out_d[l, p], SLOTW))
                for p in range(2):
                    wq_list.append((wq_d[l, p], SLOTW))
                for p in range(2):
                    wq_list.append((wo_d[l, p], SLOTW))
                for p in range(11):
                    wq_list.append((w_up_d[l, p], SLOTW))
                for p in range(4):
                    wq_list.append((w_dn_d[l, p], 3072))
                for p in range(4):
                    wq_list.append((w_dn_d[l, 4 + p, :, 0:2560], 2560))
        w_prefetch()

        def gcol(idx, kc):
            return gvec[:, idx * 8 + kc: idx * 8 + kc + 1]

        def mm_group(out_ap, bres, pairs, reads):
            n = len(pairs)
            for i, (l_, r_) in enumerate(pairs):
                OP("pe", lambda e, l_=l_, r_=r_, i=i: e.matmul(out_ap, lhsT=l_, rhs=r_, start=(i == 0), stop=(i == n - 1)),
                   reads=reads, writes=[bres], inc=(i == n - 1))

        def norm(src, srcR, gidx, dst, dstR, N, in_place_f32=False):
            for kc in range(8):
                OP("act", lambda e, kc=kc: e.activation(out=sq[:, kc, 0:N], in_=src[:, kc, 0:N], func=AF.Square),
                   reads=[srcR[kc]], writes=[sqRl[kc]])
            bk, br = nbank()
            mm_group(bk[:, 0:N], br, [(ones, sq[:, kc, 0:N]) for kc in range(8)], sqRl + [cbR])
            OP("act", lambda e: e.activation(out=rt[:, 0:N], in_=bk[:, 0:N], func=AF.Ln, bias=epsT[:, 0:1], scale=1.0 / D),
               reads=[br, cR], writes=[rtR])
            OP("act", lambda e: e.activation(out=rstd[:, 0:N], in_=rt[:, 0:N], func=AF.Exp, scale=-0.5), reads=[rtR], writes=[rstdR])
            for kc in range(8):
                eng = "dve"
                OP(eng, lambda e, kc=kc: e.scalar_tensor_tensor(out=dst[:, kc, 0:N], in0=src[:, kc, 0:N], scalar=gcol(gidx, kc),
                                                                in1=rstd[:, 0:N], op0=ALU.mult, op1=ALU.mult),
                   reads=[srcR[kc], rstdR, cR], writes=[dstR[kc]])

        rtok = sbuf("rtok", [128, 40], F32)
        rtokR = Res("rtok")
        rtokQ = [Res(f"rtokq{c}") for c in range(4)]
        rtokK = [Res(f"rtokk{c}") for c in range(4)]

        def norm_h(gidx):
            for kc in range(8):
                if kc % 2 == 0:
                    OP("dve", lambda e, kc=kc: e.tensor_scalar(out=hT[:, kc, :], in0=xT[:, kc, :], scalar1=gcol(gidx, kc), scalar2=None, op0=ALU.mult),
                       reads=[xR[kc], cR], writes=[hR[kc]])
                else:
                    OP("act", lambda e, kc=kc: e.activation(out=hT[:, kc, :], in_=xT[:, kc, :], func=AF.Copy, scale=gcol(gidx, kc)),
                       reads=[xR[kc], cR], writes=[hR[kc]])
                OP("act", lambda e, kc=kc: e.activation(out=sq[:, kc, :], in_=xT[:, kc, :], func=AF.Square), reads=[xR[kc]], writes=[sqRl[kc]])

        def norm_stats(token_major):
            bk, br = nbank()
            if token_major:
                for c in range(4):
                    for kc in range(8):
                        OP("pe", lambda e, c=c, kc=kc: e.matmul(bk[:, c:c + 1], lhsT=sq[:, kc, c * 128:(c + 1) * 128], rhs=ones[:, 0:1],
                                                                start=(kc == 0), stop=(kc == 7)),
                           reads=[sqRl[kc], cbR], writes=[br], inc=(c == 3 and kc == 7))
                OP("act", lambda e: e.activation(out=rtok[:, 0:4], in_=bk[:, 0:4], func=AF.Ln, bias=epsT[:, 0:1], scale=1.0 / D),
                   reads=[br, cR], writes=[rtokR])
                OP("act", lambda e: e.activation(out=rtok[:, 4:8], in_=rtok[:, 0:4], func=AF.Exp, scale=-0.5), reads=[rtokR], writes=[rtokR])
                for c in range(4):
                    OP("dve", lambda e, c=c: e.tensor_scalar(out=rtok[:, 8 + 4 * c:12 + 4 * c], in0=dq, scalar1=rtok[:, 4 + c:5 + c], scalar2=None, op0=ALU.mult),
                       reads=[rtokR, cR], writes=[rtokQ[c]])
                for c in range(4):
                    OP("dve", lambda e, c=c: e.tensor_scalar(out=rtok[:, 24 + 4 * c:28 + 4 * c], in0=dk, scalar1=rtok[:, 4 + c:5 + c], scalar2=None, op0=ALU.mult),
                       reads=[rtokR, cR], writes=[rtokK[c]])
            else:
                mm_group(bk[:], br, [(ones, sq[:, kc, :]) for kc in range(8)], sqRl + [cbR])
                OP("act", lambda e: e.activation(out=rt[:], in_=bk[:], func=AF.Ln, bias=epsT[:, 0:1], scale=1.0 / D), reads=[br, cR], writes=[rtR])
                OP("act", lambda e: e.activation(out=rstd[:], in_=rt[:], func=AF.Exp, scale=-0.5), reads=[rtR], writes=[rstdR])

        AR.reset()
        memT = AR.alloc(8 * NMEM, F32)
        memn = AR.alloc(8 * NMEM, BF16)
        memT3 = memT.ap.rearrange("p (k m) -> p k m", k=8)
        memn3 = memn.ap.rearrange("p (k m) -> p k m", k=8)
        S_.dma("sp", d_mem, memT3, memT_d.rearrange("(k p) m -> p k m", p=128), writes=memT.res)
        memTR = memT.res[0:8]
        memnR = [memn.res[k // 2] for k in range(8)]
        norm(memT3, memTR, 3 * NL + 1, memn3, memnR, NMEM)
        for l in range(NL):
            for p in range(4):
                sl, sr = w_get()
                sl3 = sl[:].rearrange("p (k n) -> p k n", k=8)
                if p < 2:
                    for dd in range(4):
                        dch = p * 4 + dd
                        bk, br = nbank()
                        mm_group(bk[:, 0:NMEM], br,
                                 [(sl3[:, kc, dd * 128:(dd + 1) * 128], memn3[:, kc, :]) for kc in range(8)],
                                 [sr] + memnR)
                        OP("act" if dd % 2 == 0 else "dve",
                           (lambda e, bk=bk, dch=dch: e.activation(out=KT[:, l * 8 + dch, :], in_=bk[:, 0:NMEM], func=AF.Copy)) if dd % 2 == 0 else
                           (lambda e, bk=bk, dch=dch: e.tensor_copy(out=KT[:, l * 8 + dch, :], in_=bk[:, 0:NMEM])),
                           reads=[br], writes=[kvR[l]])
                else:
                    for mch in range(2):
                        bk, br = nbank()
                        mm_group(bk[:], br,
                                 [(memn3[:, kc, mch * 128:(mch + 1) * 128], sl3[:, kc, :]) for kc in range(8)],
                                 [sr] + memnR)
                        OP("act" if mch == 0 else "dve",
                           (lambda e, bk=bk, mch=mch: e.activation(out=Vm[:, l * 2 + mch, (p - 2) * 512:(p - 1) * 512], in_=bk[:], func=AF.Copy)) if mch == 0 else
                           (lambda e, bk=bk, mch=mch: e.tensor_copy(out=Vm[:, l * 2 + mch, (p - 2) * 512:(p - 1) * 512], in_=bk[:])),
                           reads=[br], writes=[kvR[l]])
                w_prefetch()

        def sin_reduced(out_ap, ang, tmp1, tmp2i, allres):
            OP("dve", lambda e: e.tensor_scalar(out=tmp1, in0=ang, scalar1=1.0 / (2 * math.pi), scalar2=None, op0=ALU.mult),
               reads=allres, writes=allres)
            OP("dve", lambda e: e.tensor_copy(out=tmp2i, in_=tmp1), reads=allres, writes=allres)
            OP("dve", lambda e: e.tensor_copy(out=tmp1, in_=tmp2i), reads=allres, writes=allres)
            OP("dve", lambda e: e.scalar_tensor_tensor(out=ang, in0=tmp1, scalar=-2 * math.pi, in1=ang, op0=ALU.mult, op1=ALU.add),
               reads=allres, writes=allres)
            OP("dve", lambda e: e.tensor_scalar(out=tmp1, in0=ang, scalar1=math.pi, scalar2=-2 * math.pi, op0=ALU.is_gt, op1=ALU.mult),
               reads=allres, writes=allres)
            OP("dve", lambda e: e.tensor_tensor(out=ang, in0=ang, in1=tmp1, op=ALU.add), reads=allres, writes=allres)
            OP("dve", lambda e: e.tensor_scalar(out=tmp1, in0=ang, scalar1=-math.pi, scalar2=2 * math.pi, op0=ALU.is_lt, op1=ALU.mult),
               reads=allres, writes=allres)
            OP("dve", lambda e: e.tensor_tensor(out=ang, in0=ang, in1=tmp1, op=ALU.add), reads=allres, writes=allres)
            OP("dve", lambda e: e.tensor_scalar(out=ang, in0=ang, scalar1=math.pi, scalar2=-math.pi, op0=ALU.min, op1=ALU.max),
               reads=allres, writes=allres)
            OP("act", lambda e: e.activation(out=out_ap, in_=ang, func=AF.Sin), reads=allres, writes=[ropeR])

        for t in range(NT):
            gc0 = t * 4
            S_.dma("sp", d_x, xT[:], xT_d.rearrange("(k p) s -> p k s", p=128)[:, :, t * TT:(t + 1) * TT], writes=xR)

            AR.reset()
            ang = AR.alloc(256, F32)
            ang2 = AR.alloc(256, F32)
            tmp1 = AR.alloc(256, F32)
            tmp2 = AR.alloc(256, F32)
            allres = ang.res + ang2.res + tmp1.res + tmp2.res
            tmp2i = tmp2.ap.bitcast(I32)
            for c in range(4):
                OP("dve", lambda e, c=c: e.tensor_scalar(out=ang.ap[:, c * 64:(c + 1) * 64], in0=invf, scalar1=posf[:, gc0 + c:gc0 + c + 1],
                                                          scalar2=None, op0=ALU.mult), reads=[cR] + allres, writes=allres)
            OP("dve", lambda e: e.tensor_scalar(out=ang2.ap, in0=ang.ap, scalar1=math.pi / 2, scalar2=None, op0=ALU.add),
               reads=allres, writes=allres)
            sin_reduced(sinT[:].rearrange("p c f -> p (c f)"), ang.ap, tmp1.ap, tmp2i, allres)
            sin_reduced(cosT[:].rearrange("p c f -> p (c f)"), ang2.ap, tmp1.ap, tmp2i, allres)

            for l in range(NL):
                S_.label = 'M.norm'
                S_.dma("sp", d_gn, gn[:], gn_d[l], writes=[gnR])
                S_.dma("pool", d_pw[l % 2], pwb[l % 2][:], pw_d[l], writes=[pwR[l % 2]])
                norm_h(3 * l + 0)
                AR.reset()
                qT_ = [AR.alloc(512, BF16) for _ in range(4)]
                kT_ = [AR.alloc(512, BF16) for _ in range(4)]
                kt_ = [AR.alloc(512, BF16) for _ in range(4)]
                v_ = [AR.alloc(512, BF16) for _ in range(4)]
                u_ = [AR.alloc(512, BF16) for _ in range(4)]
                sg_ = [AR.alloc(512, BF16) for _ in range(4)]
                qt_ = [AR.alloc(512, BF16) for _ in range(3)]
                qs_ = [AR.alloc(512, F32) for _ in range(2)]
                tq = [AR.alloc(256, F32) for _ in range(2)]
                tk = [AR.alloc(256, F32) for _ in range(2)]
                pT_ = [AR.alloc(512, BF16) for _ in range(4)]
                Sb_ = [AR.alloc(512, BF16) for _ in range(4)]
                yr_ = [AR.alloc(512, BF16) for _ in range(2)]
                pl_ = [AR.alloc(512, BF16) for _ in range(4)]
                sgf = qs_
                ss_ = AR.alloc(16, F32)
                rh_ = AR.alloc(32, F32)

                def rope(eng, src, dstbuf, c, tt):
                    s4 = src.ap.rearrange("p (h x f) -> p h x f", h=4, x=2)
                    d4 = dstbuf.ap.rearrange("p (h x f) -> p h x f", h=4, x=2)
                    cs = cosT[:, c, :].unsqueeze(1).to_broadcast([128, 4, 64])
                    sn = sinT[:, c, :].unsqueeze(1).to_broadcast([128, 4, 64])
                    t1 = tt[0].ap.rearrange("p (h f) -> p h f", h=4)
                    t2 = tt[1].ap.rearrange("p (h f) -> p h f", h=4)
                    rr = src.res + [ropeR]
                    tr_ = tt[0].res + tt[1].res
                    OP(eng, lambda e: e.tensor_tensor(out=t1, in0=s4[:, :, 0, :], in1=cs, op=ALU.mult), reads=rr, writes=tt[0].res)
                    OP(eng, lambda e: e.tensor_tensor(out=t2, in0=s4[:, :, 1, :], in1=sn, op=ALU.mult), reads=rr, writes=tt[1].res)
                    OP(eng, lambda e: e.tensor_tensor(out=d4[:, :, 0, :], in0=t1, in1=t2, op=ALU.subtract), reads=tr_, writes=dstbuf.res)
                    OP(eng, lambda e: e.tensor_tensor(out=t1, in0=s4[:, :, 1, :], in1=cs, op=ALU.mult), reads=rr, writes=tt[0].res)
                    OP(eng, lambda e: e.tensor_tensor(out=t2, in0=s4[:, :, 0, :], in1=sn, op=ALU.mult), reads=rr, writes=tt[1].res)
                    OP(eng, lambda e: e.tensor_tensor(out=d4[:, :, 1, :], in0=t1, in1=t2, op=ALU.add), reads=tr_, writes=dstbuf.res)

                S_.label = 'M.P1'
                deferred = []

                def tr_q(c, qt):
                    def f():
                        b2, b2r = nbank()
                        b2b = b2[:].bitcast(BF16)
                        for h in range(4):
                            OP("pe", lambda e, h=h: e.transpose(b2b[:, h * 128:(h + 1) * 128], qt.ap[:, h * 128:(h + 1) * 128], ident),
                               reads=qt.res + [cbR], writes=[b2r], inc=(h == 3))
                        OP("act", lambda e: e.activation(out=qT_[c].ap, in_=b2b[:, 0:512], func=AF.Copy), reads=[b2r], writes=qT_[c].res)
                    return f

                def tr_k(c):
                    def f():
                        b2, b2r = nbank()
                        b2b = b2[:].bitcast(BF16)
                        for h in range(4):
                            OP("pe", lambda e, h=h: e.transpose(b2b[:, h * 128:(h + 1) * 128], kt_[c].ap[:, h * 128:(h + 1) * 128], ident),
                               reads=kt_[c].res + [cbR], writes=[b2r], inc=(h == 3))
                        OP("dve", lambda e: e.tensor_copy(out=kT_[c].ap, in_=b2b[:, 0:512]), reads=[b2r], writes=kT_[c].res)
                    return f

                for j in range(5):
                    sl, sr = w_get()
                    sl3 = sl[:].rearrange("p (k n) -> p k n", k=8)
                    pre = {}
                    if j == 0:
                        for c in range(2):
                            bk, br = nbank()
                            mm_group(bk[:], br, [(hT[:, kc, c * 128:(c + 1) * 128], sl3[:, kc, :]) for kc in range(8)], [sr] + hR)
                            pre[c] = (bk, br)
                        norm_stats(True)
                    for c in range(4):
                        if c in pre:
                            bk, br = pre[c]
                        else:
                            bk, br = nbank()
                            mm_group(bk[:], br, [(hT[:, kc, c * 128:(c + 1) * 128], sl3[:, kc, :]) for kc in range(8)], [sr] + hR)
                        if len(deferred) > (2 if j == 0 else 4):
                            deferred.pop(0)()
                        if j == 0 or j == 1:
                            qs = qs_[(j * 4 + c) % 2]
                            so = (8 if j == 0 else 24) + 4 * c
                            for h in range(4):
                                OP("act", lambda e, h=h, bk=bk, qs=qs, so=so: e.activation(
                                    out=qs.ap[:, h * 128:(h + 1) * 128], in_=bk[:, h * 128:(h + 1) * 128], func=AF.Copy, scale=rtok[:, so + h:so + h + 1]),
                                   reads=[br, (rtokQ if j == 0 else rtokK)[c]], writes=qs.res)
                            if j == 0:
                                qt = qt_[c % 3]
                                rope("dve", qs, qt, c, tq)
                                deferred.append(tr_q(c, qt))
                            else:
                                rope("pool", qs, kt_[c], c, tk)
                                deferred.append(tr_k(c))
                        elif j == 2:
                            OP("act", lambda e, bk=bk, c=c: e.activation(out=v_[c].ap, in_=bk[:], func=AF.Copy, scale=rtok[:, 4 + c:5 + c]),
                               reads=[br, rtokR], writes=v_[c].res)
                        elif j == 3:
                            sf = sgf[c % 2]
                            OP("act", lambda e, bk=bk, sf=sf, c=c: e.activation(out=sf.ap, in_=bk[:], func=AF.Silu, scale=rtok[:, 4 + c:5 + c]),
                               reads=[br, rtokR], writes=sf.res)
                            OP("pool", lambda e, sf=sf, c=c: e.tensor_tensor(out=sg_[c].ap, in0=sf.ap, in1=gn[:], op=ALU.mult),
                               reads=sf.res + [gnR], writes=sg_[c].res)
                        else:
                            OP("dve", lambda e, bk=bk, c=c: e.tensor_scalar(out=u_[c].ap, in0=bk[:], scalar1=rtok[:, 4 + c:5 + c], scalar2=None, op0=ALU.mult),
                               reads=[br, rtokR], writes=u_[c].res)
                    w_prefetch()
                while deferred:
                    deferred.pop(0)()

                S_.label = 'M.P2'
                yT = big
                pw = pwb[l % 2]

                def stA(c):
                    gc = gc0 + c
                    bs, bsr = nbank()
                    for h in range(4):
                        OP("pe", lambda e, h=h: e.matmul(bs[:, h * 128:(h + 1) * 128], lhsT=kT_[c].ap[:, h * 128:(h + 1) * 128],
                                                         rhs=qT_[c].ap[:, h * 128:(h + 1) * 128], start=True, stop=True),
                           reads=kT_[c].res + qT_[c].res, writes=[bsr], inc=(h == 3))
                    pT = pT_[c]
                    OP("dve", lambda e: e.tensor_tensor(out=pT.ap.rearrange("p (h i) -> p h i", h=4), in0=bs[:].rearrange("p (h i) -> p h i", h=4),
                                                        in1=maskf.unsqueeze(1).to_broadcast([128, 4, 128]), op=ALU.mult),
                       reads=[bsr, cR], writes=pT.res)
                    bkv, bkvr = nbank()
                    for h in range(4):
                        OP("pe", lambda e, h=h: e.matmul(bkv[:, h * 128:(h + 1) * 128], lhsT=kt_[c].ap[:, h * 128:(h + 1) * 128],
                                                         rhs=v_[c].ap[:, h * 128:(h + 1) * 128], start=True, stop=True),
                           reads=kt_[c].res + v_[c].res, writes=[bkvr], inc=(h == 3))
                    Sb = Sb_[c]
                    for h in range(4):
                        OP("dve", lambda e, h=h: e.scalar_tensor_tensor(out=Tst[:, l, h * 128:(h + 1) * 128], in0=Tst[:, l, h * 128:(h + 1) * 128],
                                                                        scalar=GAMC[h], in1=bkv[:, h * 128:(h + 1) * 128], op0=ALU.mult, op1=ALU.add),
                           reads=[bkvr, stT[l]], writes=[stT[l]])
                    for h in range(4):
                        OP("dve", lambda e, h=h: e.tensor_scalar(out=Sb.ap[:, h * 128:(h + 1) * 128], in0=Tst[:, l, h * 128:(h + 1) * 128],
                                                                 scalar1=GAMC[h], scalar2=None, op0=ALU.mult),
                           reads=[stT[l]], writes=Sb.res)
                    bp, bpr = nbank()
                    for g in range(4):
                        if gc == 0:
                            OP("pe", lambda e, g=g: e.matmul(bp[:, g * 128:(g + 1) * 128], lhsT=u_[c].ap[:, g * 128:(g + 1) * 128],
                                                             rhs=band0b[:, g * 128:(g + 1) * 128], start=True, stop=True),
                               reads=u_[c].res + [cbR], writes=[bpr], inc=(g == 3))
                        else:
                            if c == 0:
                                up_ap, up_res = uprev[:, l, :], [stU[l]]
                            else:
                                up_ap, up_res = u_[c - 1].ap, u_[c - 1].res
                            OP("pe", lambda e, g=g: e.matmul(bp[:, g * 128:(g + 1) * 128], lhsT=u_[c].ap[:, g * 128:(g + 1) * 128],
                                                             rhs=bandb[:, g * 128:(g + 1) * 128], start=True, stop=False),
                               reads=u_[c].res + [cbR], writes=[bpr], inc=False)
                            OP("pe", lambda e, g=g, up_ap=up_ap: e.matmul(bp[:, g * 128:(g + 1) * 128], lhsT=up_ap[:, g * 128:(g + 1) * 128],
                                                                          rhs=bandpb[:, g * 128:(g + 1) * 128], start=False, stop=True),
                               reads=up_res + [cbR], writes=[bpr], inc=(g == 3))
                    pl = pl_[c]
                    OP("act", lambda e: e.activation(out=pl.ap, in_=bp[:], func=AF.Copy), reads=[bpr], writes=pl.res)

                def stB(c):
                    pT = pT_[c]
                    bo, bor = nbank()
                    if c == 0:
                        Sprev_ap, Sprev_res = SbfL[:, l, :], [stS[l]]
                    else:
                        Sprev_ap, Sprev_res = Sb_[c - 1].ap, Sb_[c - 1].res
                    for h in range(4):
                        OP("pe", lambda e, h=h: e.matmul(bo[:, h * 128:(h + 1) * 128], lhsT=pT.ap[:, h * 128:(h + 1) * 128],
                                                         rhs=v_[c].ap[:, h * 128:(h + 1) * 128], start=True, stop=False),
                           reads=pT.res + v_[c].res, writes=[bor], inc=False)
                        OP("pe", lambda e, h=h: e.matmul(bo[:, h * 128:(h + 1) * 128], lhsT=qT_[c].ap[:, h * 128:(h + 1) * 128],
                                                         rhs=Sprev_ap[:, h * 128:(h + 1) * 128], start=False, stop=True),
                           reads=qT_[c].res + Sprev_res, writes=[bor], inc=(h == 3))
                    for h in range(4):
                        OP("act", lambda e, h=h: e.activation(out=sgf[0].ap[:, h * 128:(h + 1) * 128], in_=bo[:, h * 128:(h + 1) * 128],
                                                              func=AF.Square, accum_out=ss_.ap[:, 4 * c + h:4 * c + h + 1]),
                           reads=[bor], writes=sgf[0].res + ss_.res)
                    OP("act", lambda e: e.activation(out=rh_.ap[:, 8 * c:8 * c + 4], in_=ss_.ap[:, 4 * c:4 * c + 4], func=AF.Ln, bias=epsT[:, 0:1], scale=1.0 / 128),
                       reads=ss_.res + [cR], writes=rh_.res)
                    OP("act", lambda e: e.activation(out=rh_.ap[:, 8 * c + 4:8 * c + 8], in_=rh_.ap[:, 8 * c:8 * c + 4], func=AF.Exp, scale=-0.5),
                       reads=rh_.res, writes=rh_.res)
                    yr = yr_[c % 2]
                    for h in range(4):
                        OP("dve", lambda e, h=h: e.scalar_tensor_tensor(out=yr.ap[:, h * 128:(h + 1) * 128], in0=bo[:, h * 128:(h + 1) * 128],
                                                                        scalar=rh_.ap[:, 8 * c + 4 + h:8 * c + 5 + h], in1=sg_[c].ap[:, h * 128:(h + 1) * 128],
                                                                        op0=ALU.mult, op1=ALU.mult),
                           reads=[bor] + rh_.res + sg_[c].res, writes=yr.res)
                    pl = pl_[c]
                    bq, bqr = nbank()
                    for g in range(4):
                        OP("pe", lambda e, g=g: e.matmul(bq[:, g * 128:(g + 1) * 128], lhsT=pw[:, g * 128:(g + 1) * 128],
                                                         rhs=pl.ap[:, g * 128:(g + 1) * 128], start=True, stop=True),
                           reads=pl.res + [pwR[l % 2]], writes=[bqr], inc=(g == 3))
                    OP("dve", lambda e: e.tensor_tensor(out=yT[:, 4:8, c * 128:(c + 1) * 128],
                                                        in0=bq[:].rearrange("p (g t) -> p g t", g=4),
                                                        in1=psc[:, l * 4:(l + 1) * 4].unsqueeze(2).to_broadcast([128, 4, 128]), op=ALU.mult),
                       reads=[bqr, cR], writes=BRh(4, 8, c // 2))

                def stC(c):
                    yr = yr_[c % 2]
                    by, byr = nbank()
                    byb = by[:].bitcast(BF16)
                    for h in range(4):
                        OP("pe", lambda e, h=h: e.transpose(byb[:, h * 128:(h + 1) * 128], yr.ap[:, h * 128:(h + 1) * 128], ident),
                           reads=yr.res + [cbR], writes=[byr], inc=(h == 3))
                    OP("act", lambda e: e.activation(out=yT[:, 0:4, c * 128:(c + 1) * 128],
                                                     in_=byb[:, 0:512].rearrange("p (h t) -> p h t", h=4), func=AF.Copy),
                       reads=[byr], writes=BRh(0, 4, c // 2))

                wsl = [w_get(), w_get()]
                wsl3 = [(a[:].rearrange("p (k n) -> p k n", k=8), r) for a, r in wsl]

                def p3(dm, half):
                    t0_, t1_ = half * 256, (half + 1) * 256
                    bk, br = nbank()
                    w3, wr = wsl3[dm // 4]
                    mm_group(bk[:, 0:256], br, [(w3[:, k, (dm % 4) * 128:(dm % 4 + 1) * 128], yT[:, k, t0_:t1_]) for k in range(8)],
                             BRh(0, 8, half) + [wr])
                    OP("dve", lambda e: e.tensor_tensor(out=xT[:, dm, t0_:t1_], in0=bk[:, 0:256], in1=xT[:, dm, t0_:t1_], op=ALU.add),
                       reads=[br, xR[dm]], writes=[xR[dm]])

                stA(0); stA(1); stB(0); stA(2); stB(1); stC(0); stA(3); stB(2); stC(1)
                S_.label = 'M.P3'
                p3(0, 0); p3(1, 0); p3(2, 0)
                S_.label = 'M.P2'
                stB(3)
                S_.label = 'M.P3'
                p3(3, 0); p3(4, 0); p3(5, 0)
                S_.label = 'M.P2'
                stC(2)
                S_.label = 'M.P3'
                p3(6, 0); p3(7, 0)
                S_.label = 'M.P2'
                stC(3)
                OP("pool", lambda e: e.tensor_copy(out=SbfL[:, l, :], in_=Sb_[3].ap), reads=Sb_[3].res, writes=[stS[l]])
                OP("pool", lambda e: e.tensor_copy(out=uprev[:, l, :], in_=u_[3].ap), reads=u_[3].res, writes=[stU[l]])

                S_.label = 'M.P3'
                def proj_residual(src, srcR, nk, lhs_of):
                    for dm in range(8):
                        bk, br = nbank()
                        prs, rds = [], list(srcR)
                        for k in range(nk):
                            lh, lr = lhs_of(dm, k)
                            prs.append((lh, src[:, k, :]))
                            if lr not in rds:
                                rds.append(lr)
                        mm_group(bk[:], br, prs, rds)
                        OP("dve", lambda e, dm=dm, bk=bk: e.tensor_tensor(out=xT[:, dm, :], in0=bk[:], in1=xT[:, dm, :], op=ALU.add),
                           reads=[br, xR[dm]], writes=[xR[dm]])

                for dm in range(8):
                    p3(dm, 1)
                w_prefetch()

                S_.label = 'X'
                norm_h(3 * l + 1)
                AR.reset()
                pe_ = [AR.alloc(1024, BF16) for _ in range(3)]
                rd_ = [AR.alloc(512, F32) for _ in range(2)]
                qTb = big
                wsl = [w_get(), w_get()]
                wsl3 = [(a[:].rearrange("p (k n) -> p k n", k=8), r) for a, r in wsl]
                pre = {}
                for dch in range(2):
                    bk, br = nbank()
                    w3, wr = wsl3[dch // 4]
                    mm_group(bk[:], br, [(w3[:, kc, (dch % 4) * 128:(dch % 4 + 1) * 128], hT[:, kc, :]) for kc in range(8)], [wr] + hR)
                    pre[dch] = (bk, br)
                norm_stats(False)
                for dch in range(8):
                    w3, wr = wsl3[dch // 4]
                    if dch in pre:
                        bk, br = pre[dch]
                    else:
                        bk, br = nbank()
                        mm_group(bk[:], br, [(w3[:, kc, (dch % 4) * 128:(dch % 4 + 1) * 128], hT[:, kc, :]) for kc in range(8)], [wr] + hR)
                    OP("dve", lambda e, bk=bk, dch=dch: e.tensor_tensor(out=qTb[:, 8 + dch, :], in0=bk[:], in1=rstd[:], op=ALU.mult),
                       reads=[br, rstdR], writes=[bR[8 + dch]])
                w_prefetch()
                def xS(h):
                    pe = pe_[h % 3]
                    for mch in range(2):
                        bk, br = nbank()
                        mm_group(bk[:], br, [(KT[:, l * 8 + 2 * h + dd, mch * 128:(mch + 1) * 128], qTb[:, 8 + 2 * h + dd, :]) for dd in range(2)],
                                 [kvR[l], bR[8 + 2 * h], bR[9 + 2 * h]])
                        OP("act", lambda e, bk=bk, mch=mch: e.activation(out=pe.ap[:, mch * 512:(mch + 1) * 512], in_=bk[:], func=AF.Exp, scale=1.0 / 16),
                           reads=[br], writes=pe.res)

                def xD(h):
                    pe = pe_[h % 3]
                    rd = rd_[h % 2]
                    bk, br = nbank()
                    mm_group(bk[:], br, [(ones, pe.ap[:, mch * 512:(mch + 1) * 512]) for mch in range(2)], pe.res + [cbR])
                    OP("act", lambda e: e.activation(out=rd.ap, in_=bk[:], func=AF.Ln), reads=[br], writes=rd.res)
                    OP("act", lambda e: e.activation(out=rd.ap, in_=rd.ap, func=AF.Exp, scale=-1.0), reads=rd.res, writes=rd.res)
                    for dd in range(2):
                        bk2, br2 = nbank()
                        mm_group(bk2[:], br2, [(Vm[:, l * 2 + mch, (2 * h + dd) * 128:(2 * h + dd + 1) * 128], pe.ap[:, mch * 512:(mch + 1) * 512]) for mch in range(2)],
                                 pe.res + [kvR[l]])
                        OP("dve", lambda e, bk2=bk2, dd=dd: e.tensor_tensor(out=big[:, 2 * h + dd, :], in0=bk2[:], in1=rd.ap, op=ALU.mult),
                           reads=[br2] + rd.res, writes=BR(2 * h + dd))

                xS(0); xS(1); xD(0); xS(2); xD(1); xS(3); xD(2); xD(3)
                wsl = [w_get(), w_get()]
                wsl3 = [(a[:].rearrange("p (k n) -> p k n", k=8), r) for a, r in wsl]
                proj_residual(big, BR(0, 8), 8, lambda dm, k: (wsl3[dm // 4][0][:, k, (dm % 4) * 128:(dm % 4 + 1) * 128], wsl3[dm // 4][1]))
                w_prefetch()

                S_.label = 'F.norm'
                norm(xT, xR, 3 * l + 2, hT, hR, TT)
                AR.reset()
                S_.label = 'F.up'
                ab_ = [AR.alloc(TT + 2, F32) for _ in range(8)]
                cc_ = [AR.alloc(TT, F32) for _ in range(8)]
                pend = []
                sf_ = [AR.alloc(TT, F32) for _ in range(3)]
                cwl = lambda tap, ch: cw[:, (l * 4 + tap) * 44 + ch:(l * 4 + tap) * 44 + ch + 1]
                it = 0
                chs = [2 * j + e_ + 22 * gv for j in range(11) for e_ in range(2) for gv in range(2)]

                def tail_in(i):
                    if i < len(chs):
                        ch_, ab_i = chs[i], ab_[i % 8]
                        OP("pool", lambda e: e.tensor_copy(out=ab_i.ap[:, 0:2], in_=atail[:, l * 44 + ch_, :]), reads=[tailR[l][ch_]], writes=ab_i.res)

                tail_in(0)
                tail_in(1)
                for j in range(11):
                    sl, sr = w_get()
                    sl3 = sl[:].rearrange("p (k n) -> p k n", k=8)
                    for e_ in range(2):
                        fc = 2 * j + e_
                        cbufs = []
                        tail_in(it + 2)
                        tail_in(it + 3)
                        for gv in range(2):
                            ch = fc + 22 * gv
                            col0 = gv * 256 + e_ * 128
                            bk, br = nbank()
                            mm_group(bk[:], br, [(sl3[:, kc, col0:col0 + 128], hT[:, kc, :]) for kc in range(8)], [sr] + hR)
                            ab = ab_[it % 8]
                            cc = cc_[it % 8]
                            it += 1
                            tr = tailR[l][ch]
                            assert chs[it - 1] == ch
                            OP("act", lambda e, ab=ab, bk=bk: e.activation(out=ab.ap[:, 2:TT + 2], in_=bk[:], func=AF.Copy), reads=[br], writes=ab.res)
                            OP("act", lambda e, cc=cc, bk=bk, ch=ch: e.activation(out=cc.ap, in_=bk[:], func=AF.Identity, scale=cwl(2, ch), bias=cwl(3, ch)),
                               reads=[br, cR], writes=cc.res)
                            OP("pool", lambda e, ab=ab, ch=ch: e.tensor_copy(out=atail[:, l * 44 + ch, :], in_=ab.ap[:, TT:TT + 2]), reads=ab.res, writes=[tr])
                            OP("dve", lambda e, ab=ab, cc=cc, ch=ch: e.scalar_tensor_tensor(out=cc.ap, in0=ab.ap[:, 1:TT + 1], scalar=cwl(1, ch), in1=cc.ap,
                                                                                            op0=ALU.mult, op1=ALU.add), reads=ab.res + cc.res + [cR], writes=cc.res)
                            OP("dve", lambda e, ab=ab, cc=cc, ch=ch: e.scalar_tensor_tensor(out=cc.ap, in0=ab.ap[:, 0:TT], scalar=cwl(0, ch), in1=cc.ap,
                                                                                            op0=ALU.mult, op1=ALU.add), reads=ab.res + cc.res + [cR], writes=cc.res)
                            cbufs.append(cc)
                        def fin(fc=fc, cg=cbufs[0], cv=cbufs[1]):
                            sf = sf_[fc % 3]
                            OP("act", lambda e: e.activation(out=sf.ap, in_=cg.ap, func=AF.Silu), reads=cg.res, writes=sf.res)
                            OP("pool", lambda e: e.tensor_tensor(out=big[:, fc, :], in0=sf.ap, in1=cv.ap, op=ALU.mult),
                               reads=sf.res + cv.res, writes=BR(fc))
                        pend.append(fin)
                        if len(pend) > 2:
                            pend.pop(0)()
                    w_prefetch()
                while pend:
                    pend.pop(0)()
                S_.label = 'F.down'
                for half, (f0, f1) in enumerate(((0, 12), (12, NFC))):
                    W_ = (f1 - f0) * 128
                    for dmp in range(4):
                        sl, sr = w_get()
                        for dd in range(2):
                            dm = 2 * dmp + dd
                            bk, br = nbank()
                            mm_group(bk[:], br, [(sl[:, dd * W_ + (fc - f0) * 128:dd * W_ + (fc - f0 + 1) * 128], big[:, fc, :]) for fc in range(f0, f1)],
                                     [sr] + BR(f0, f1))
                            OP("dve", lambda e, dm=dm, bk=bk: e.tensor_tensor(out=xT[:, dm, :], in0=bk[:], in1=xT[:, dm, :], op=ALU.add),
                               reads=[br, xR[dm]], writes=[xR[dm]])
                        w_prefetch()

            S_.label = 'final'
            if final_norm:
                for kc in range(8):
                    OP("act", lambda e, kc=kc: e.activation(out=sq[:, kc, :], in_=xT[:, kc, :], func=AF.Square), reads=[xR[kc]], writes=[sqRl[kc]])
                bk, br = nbank()
                mm_group(bk[:], br, [(ones, sq[:, kc, :]) for kc in range(8)], sqRl + [cbR])
                OP("act", lambda e: e.activation(out=rt[:], in_=bk[:], func=AF.Ln, bias=epsT[:, 0:1], scale=1.0 / D), reads=[br, cR], writes=[rtR])
                OP("act", lambda e: e.activation(out=rstd[:], in_=rt[:], func=AF.Exp, scale=-0.5), reads=[rtR], writes=[rstdR])
                for kc in range(8):
                    OP("dve",
                       lambda e, kc=kc: e.scalar_tensor_tensor(out=xT[:, kc, :], in0=xT[:, kc, :], scalar=gcol(3 * NL, kc), in1=rstd[:],
                                                               op0=ALU.mult, op1=ALU.mult), reads=[xR[kc], rstdR, cR], writes=[xR[kc]])
            S_.dma("sp", d_out, out_d.rearrange("(k p) s -> p k s", p=128)[:, :, t * TT:(t + 1) * TT], xT[:], reads=xR)

        de = S_.engs[d_out]
        S_.engs["sp"].handle.wait_ge(de.sem, de.cnt * 16)
        build.stats = (S_.ninstr, S_.nwaits)
        build.labels = S_.labels
    return nc


def _slot_layout(w):
    K, N = w.shape
    n = N // 512
    a = w.reshape(8, 128, n, 512).transpose(2, 1, 0, 3)
    return np.ascontiguousarray(a.reshape(n, 128, 4096))


def prep_shared(NL, mix_norm_g, w_in, ret_gn_g, pool_w, pool_scale, w_out, xa_norm_g, mem_norm_g, xa_wq, xa_wkv, xa_wo,
                ffn_norm_g, ffn_w_up, ffn_conv_w, ffn_conv_b, ffn_w_down, final_norm_g):
    f = lambda a: np.asarray(a, np.float32)
    cblob, _ = _consts()
    gv = []
    for l in range(NL):
        for g in (mix_norm_g, xa_norm_g, ffn_norm_g):
            gv.append(f(g)[l].reshape(8, 128).T)
    gv.append(f(final_norm_g).reshape(8, 128).T)
    gv.append(f(mem_norm_g).reshape(8, 128).T)
    gvec = np.ascontiguousarray(np.concatenate(gv, 1))
    gn = np.ascontiguousarray(np.broadcast_to(f(ret_gn_g)[:NL, None, :], (NL, 128, 512)))
    psc = np.ascontiguousarray(np.concatenate([f(pool_scale)[l].reshape(4, 128).T for l in range(NL)], 1))
    cws = []
    for l in range(NL):
        for tap in range(3):
            cws.append(f(ffn_conv_w)[l, tap].reshape(44, 128).T)
        cws.append(f(ffn_conv_b)[l].reshape(44, 128).T)
    cw = np.ascontiguousarray(np.concatenate(cws, 1))
    w_in_r = np.stack([_slot_layout(f(w_in)[l]) for l in range(NL)])
    w_out_r = np.stack([_slot_layout(f(w_out)[l]) for l in range(NL)])
    wq_r = np.stack([_slot_layout(f(xa_wq)[l]) for l in range(NL)])
    wo_r = np.stack([_slot_layout(f(xa_wo)[l]) for l in range(NL)])
    wkv_r = np.stack([_slot_layout(f(xa_wkv)[l]) for l in range(NL)])
    ups = []
    for l in range(NL):
        wu = f(ffn_w_up)[l]
        cols = []
        for j in range(11):
            cols.append(wu[:, 256 * j:256 * j + 256])
            cols.append(wu[:, DFF + 256 * j:DFF + 256 * j + 256])
        ups.append(_slot_layout(np.concatenate(cols, 1)))
    w_up_r = np.stack(ups)
    dn = []
    for l in range(NL):
        a = f(ffn_w_down)[l].reshape(NFC, 128, 8, 128).transpose(2, 1, 0, 3)
        pa = a[:, :, 0:12, :].reshape(4, 2, 128, 1536).transpose(0, 2, 1, 3).reshape(4, 128, 3072)
        pb = np.zeros((4, 128, 3072), np.float32)
        pb[:, :, 0:2560] = a[:, :, 12:NFC, :].reshape(4, 2, 128, 1280).transpose(0, 2, 1, 3).reshape(4, 128, 2560)
        dn.append(np.concatenate([pa, pb], 0))
    w_dn_r = np.stack(dn)
    pw_r = np.stack([np.ascontiguousarray(f(pool_w)[l].transpose(1, 0, 2)).reshape(128, 512) for l in range(NL)])
    return {"consts": cblob, "gvec": gvec, "gn": gn, "psc": psc, "cw": cw, "w_in": w_in_r, "w_out": w_out_r, "wq": wq_r,
            "wo": wo_r, "wkv": wkv_r, "w_up": w_up_r, "w_dn": w_dn_r, "pw": pw_r}


def prep_core(x_b, mem_b, pos_b):
    S = x_b.shape[0]
    return {"xT": np.ascontiguousarray(np.asarray(x_b, np.float32).T),
            "memT": np.ascontiguousarray(np.asarray(mem_b, np.float32).T),
            "pos": np.ascontiguousarray(np.asarray(pos_b, np.int32).reshape(S // 128, 128).T)}


_NC_CACHE = {}


def run_cores(x, mem, positions, params, NL, final_norm=True):
    B, S, _ = x.shape
    key = (S, NL, final_norm)
    if key not in _NC_CACHE:
        _NC_CACHE[key] = build(S, NL, final_norm)
    nc = _NC_CACHE[key]
    shared = prep_shared(NL, **params)
    in_maps = []
    for b in range(B):
        m = dict(shared)
        m.update(prep_core(x[b], mem[b], positions[b]))
        in_maps.append(m)
    res = run_bass_kernel_spmd(nc, in_maps, core_ids=list(range(B)))
    return np.stack([np.ascontiguousarray(r["out"].T) for r in res.results]).astype(np.float32)


def kernel(x, mem, positions, mix_norm_g, w_in, ret_gn_g, pool_w, pool_scale, w_out, xa_norm_g, mem_norm_g, xa_wq,
           xa_wkv, xa_wo, ffn_norm_g, ffn_w_up, ffn_conv_w, ffn_conv_b, ffn_w_down, final_norm_g):
    params = dict(mix_norm_g=mix_norm_g, w_in=w_in, ret_gn_g=ret_gn_g, pool_w=pool_w, pool_scale=pool_scale, w_out=w_out,
                  xa_norm_g=xa_norm_g, mem_norm_g=mem_norm_g, xa_wq=xa_wq, xa_wkv=xa_wkv, xa_wo=xa_wo, ffn_norm_g=ffn_norm_g,
                  ffn_w_up=ffn_w_up, ffn_conv_w=ffn_conv_w, ffn_conv_b=ffn_conv_b, ffn_w_down=ffn_w_down,
                  final_norm_g=final_norm_g)
    x = np.asarray(x, np.float32)
    mem = np.asarray(mem, np.float32)
    positions = np.asarray(positions)
    NL = np.asarray(w_in).shape[0]
    if FUSED:
        return run_cores(x, mem, positions, params, NL, True)
    per_layer = ("mix_norm_g", "w_in", "ret_gn_g", "pool_w", "pool_scale", "w_out", "xa_norm_g", "xa_wq", "xa_wkv", "xa_wo",
                 "ffn_norm_g", "ffn_w_up", "ffn_conv_w", "ffn_conv_b", "ffn_w_down")
    cur = x
    for l in range(NL):
        pl = {k: (np.asarray(v)[l:l + 1] if k in per_layer else v) for k, v in params.items()}
        cur = run_cores(cur, mem, positions, pl, 1, l == NL - 1)
    return cur
```
